# Optimizing a Trainium2 kernel written in Bass

```python
import math
import jax, jax.numpy as jnp
from jax import lax
import numpy as np

D_MODEL = 1024
BATCH = 2
SEQ = 8192
DEPTH = 4
DEC_BATCH = 4
DEC_SEQ = 4096
PAST_LEN = 128

ATT_HEADS = 4
ATT_DK = 64
ATT_DV = 2 * ATT_DK
Q_W = ATT_HEADS * 2 * ATT_DK
K_W = ATT_HEADS * 2 * ATT_DK
ATT_W = ATT_HEADS * ATT_DV
Q_BLOCK = 128
SGU_GROUPS = 4
SGU_CHUNK = 128
SGU_GC = 128
SGU_W = SGU_GROUPS * SGU_GC
S5_GC = 16
S5_GROUPS = 32
S5_N = 64
S5_W = S5_GROUPS * S5_GC
DT_MIN = 0.001
DT_MAX = 0.1
N_BRANCH = 3
IN_SPLITS = (Q_W, Q_W + K_W, Q_W + K_W + ATT_W, Q_W + K_W + ATT_W + 2 * SGU_W,
             Q_W + K_W + ATT_W + 2 * SGU_W + S5_W)
IN_W = IN_SPLITS[-1] + N_BRANCH * D_MODEL
PEER_HEADS = 8
PEER_NKEYS = 128
PEER_EXPERTS = PEER_NKEYS * PEER_NKEYS
PEER_DQ = 256
PEER_DHALF = PEER_DQ // 2
PEER_TOPK = 16
PEER_BLOCK = 128
ALPHA = (2 * DEPTH) ** 0.25
BETA = (8 * DEPTH) ** -0.25
LN_EPS = 1e-5

kernel_name = 'hybrid_diffattn_sgu_s5_peer_encoder'


def layernorm(x, g, b):
    xf = x.astype(jnp.float32)
    mu = jnp.mean(xf, axis=-1, keepdims=True)
    xc = xf - mu
    var = jnp.mean(xc * xc, axis=-1, keepdims=True)
    return (xc * lax.rsqrt(var + LN_EPS) * g.astype(jnp.float32) + b.astype(jnp.float32)).astype(x.dtype)


def rmsnorm(x, g):
    xf = x.astype(jnp.float32)
    return (xf * lax.rsqrt(jnp.mean(xf * xf, axis=-1, keepdims=True) + LN_EPS) * g.astype(jnp.float32)).astype(x.dtype)


def diff_attention(q, k, v, lam):
    B, L = q.shape[0], q.shape[1]
    nblk = L // Q_BLOCK
    slopes = jnp.exp2(-8.0 * jnp.arange(1, ATT_HEADS + 1, dtype=jnp.float32) / ATT_HEADS)
    q = q * (ATT_DK ** -0.5)
    qb = q.reshape(B, nblk, Q_BLOCK, ATT_HEADS, 2, ATT_DK).transpose(1, 0, 2, 3, 4, 5)
    offs = jnp.arange(nblk, dtype=jnp.int32) * Q_BLOCK
    kpos = jnp.arange(L, dtype=jnp.int32)

    def one_block(args):
        qblk, off = args
        s = jnp.einsum('bqhcd,bkhcd->bhcqk', qblk, k).astype(jnp.float32)
        qpos = off + jnp.arange(Q_BLOCK, dtype=jnp.int32)
        dist = jnp.abs(qpos[:, None] - kpos[None, :]).astype(jnp.float32)
        s = s - slopes[None, :, None, None, None] * dist[None, None, None]
        p = jax.nn.softmax(s, axis=-1)
        w = (p[:, :, 0] - lam * p[:, :, 1]).astype(v.dtype)
        return jnp.einsum('bhqk,bkhe->bqhe', w, v)

    o = lax.map(one_block, (qb, offs))
    return o.transpose(1, 0, 2, 3, 4).reshape(B, L, ATT_HEADS, ATT_DV)


def spatial_gating(z, ln_g, ln_b, w_s, b_s):
    B, L = z.shape[0], z.shape[1]
    z = jax.nn.gelu(z)
    u, v = jnp.split(z, 2, axis=-1)
    v = layernorm(v, ln_g, ln_b)
    vb = v.reshape(B, L // SGU_CHUNK, SGU_CHUNK, SGU_GROUPS, SGU_GC)
    sv = jnp.einsum('gts,bnsgc->bntgc', w_s, vb) + b_s.T[None, None, :, :, None]
    return u * sv.reshape(B, L, SGU_W)


def _cplx_combine(c1, c2):
    a1r, a1i, b1r, b1i = c1
    a2r, a2i, b2r, b2i = c2
    return (a2r * a1r - a2i * a1i,
            a2r * a1i + a2i * a1r,
            a2r * b1r - a2i * b1i + b2r,
            a2r * b1i + a2i * b1r + b2i)


def s5_bidirectional(u, a_re, a_im, log_step, b_re, b_im, c_re, c_im, d_skip, glu_w, glu_b):
    B, L = u.shape[0], u.shape[1]
    uf = u.astype(jnp.float32).reshape(B, L, S5_GROUPS, S5_GC)
    y = uf * d_skip.astype(jnp.float32).reshape(S5_GROUPS, S5_GC)
    for dirn in range(2):
        lr = a_re[dirn].astype(jnp.float32)
        li = a_im[dirn].astype(jnp.float32)
        dt = jnp.exp(log_step[dirn].astype(jnp.float32))[:, None]
        mag = jnp.exp(lr * dt)
        abr = mag * jnp.cos(li * dt)
        abi = mag * jnp.sin(li * dt)
        den = lr * lr + li * li
        nr = abr - 1.0
        fr = (nr * lr + abi * li) / den
        fi = (abi * lr - nr * li) / den
        br = b_re[dirn].astype(jnp.float32)
        bi = b_im[dirn].astype(jnp.float32)
        bbr = fr[..., None] * br - fi[..., None] * bi
        bbi = fr[..., None] * bi + fi[..., None] * br
        bur = jnp.einsum('blgc,gnc->blgn', uf, bbr)
        bui = jnp.einsum('blgc,gnc->blgn', uf, bbi)
        afr = jnp.broadcast_to(abr, bur.shape)
        afi = jnp.broadcast_to(abi, bur.shape)
        _, _, xr, xi = lax.associative_scan(_cplx_combine, (afr, afi, bur, bui),
                                            reverse=(dirn == 1), axis=1)
        y = y + jnp.einsum('blgn,gcn->blgc', xr, c_re[dirn].astype(jnp.float32)) \
              - jnp.einsum('blgn,gcn->blgc', xi, c_im[dirn].astype(jnp.float32))
    y = jax.nn.gelu(y.reshape(B, L, S5_W).astype(u.dtype))
    return y * jax.nn.sigmoid(jnp.einsum('ble,ef->blf', y, glu_w) + glu_b)


def peer(x, w_q, keys, u_tab, v_tab):
    B, L, D = x.shape
    xb = x.reshape(-1, PEER_BLOCK, D)

    def one_block(xt):
        P = xt.shape[0]
        q = jnp.einsum('pd,de->pe', xt, w_q).reshape(P, PEER_HEADS, 2, PEER_DHALF)
        s = jnp.einsum('phcd,hckd->phck', q, keys)
        sv, si = lax.top_k(s, PEER_TOPK)
        cand = (sv[:, :, 0, :, None] + sv[:, :, 1, None, :]).reshape(P, PEER_HEADS, PEER_TOPK * PEER_TOPK)
        sc, flat = lax.top_k(cand, PEER_TOPK)
        e1 = jnp.take_along_axis(si[:, :, 0], flat // PEER_TOPK, axis=-1)
        e2 = jnp.take_along_axis(si[:, :, 1], flat % PEER_TOPK, axis=-1)
        e = e1 * PEER_NKEYS + e2
        g = jax.nn.softmax(sc.astype(jnp.float32), axis=-1).astype(xt.dtype)
        hid = jnp.einsum('phkd,pd->phk', u_tab[e], xt)
        return jnp.einsum('phk,phkd->pd', g * jax.nn.gelu(hid), v_tab[e])

    return lax.map(one_block, xb).reshape(B, L, D)


def encoder_layer(x, lam_init, w_in, b_gate, lambda_q1, lambda_k1, lambda_q2, lambda_k2, att_norm_g,
                  sgu_ln_g, sgu_ln_b, sgu_w_s, sgu_b_s,
                  s5_a_re, s5_a_im, s5_log_step, s5_b_re, s5_b_im, s5_c_re, s5_c_im, s5_d, s5_glu_w, s5_glu_b,
                  w_branch, w_o, ln1_g, ln1_b, peer_w_q, peer_keys, peer_u, peer_v, ln2_g, ln2_b):
    B, L, D = x.shape
    h = jnp.einsum('bld,de->ble', x, w_in)
    q, k, v, z_sgu, u_s5, g_logit = jnp.split(h, IN_SPLITS, axis=-1)
    lam = (jnp.exp(jnp.sum(lambda_q1.astype(jnp.float32) * lambda_k1.astype(jnp.float32)))
           - jnp.exp(jnp.sum(lambda_q2.astype(jnp.float32) * lambda_k2.astype(jnp.float32))) + lam_init)
    att = diff_attention(q.reshape(B, L, ATT_HEADS, 2, ATT_DK), k.reshape(B, L, ATT_HEADS, 2, ATT_DK),
                         v.reshape(B, L, ATT_HEADS, ATT_DV), lam)
    att = (rmsnorm(att, att_norm_g) * (1.0 - lam_init)).reshape(B, L, ATT_W)
    sgu = spatial_gating(z_sgu, sgu_ln_g, sgu_ln_b, sgu_w_s, sgu_b_s)
    s5 = s5_bidirectional(u_s5, s5_a_re, s5_a_im, s5_log_step, s5_b_re, s5_b_im, s5_c_re, s5_c_im,
                          s5_d, s5_glu_w, s5_glu_b)
    gates = jax.nn.sigmoid(g_logit.reshape(B, L, N_BRANCH, D) + b_gate)
    branches = jnp.stack([att, sgu, s5], axis=2)
    proj = jnp.einsum('blie,ied->blid', branches, w_branch)
    mixed = jnp.einsum('bld,de->ble', jnp.sum(gates * proj, axis=2), w_o)
    x = layernorm(ALPHA * x + mixed, ln1_g, ln1_b)
    x = layernorm(ALPHA * x + peer(x, peer_w_q, peer_keys, peer_u, peer_v), ln2_g, ln2_b)
    return x


def setup_inputs(seed: int = 0) -> dict:
    key = jax.random.key(seed)
    ks = jax.random.split(key, 40)
    f32 = jnp.float32

    def nrm(i, shape, scale):
        return jax.random.normal(ks[i], shape, f32) * scale

    a_im_base = math.pi * jnp.arange(S5_N, dtype=f32)
    return {
        'x_prompt': nrm(0, (BATCH, SEQ, D_MODEL), 1.0),
        'x_sample': nrm(1, (DEC_BATCH, DEC_SEQ, D_MODEL), 1.0),
        'w_in': nrm(2, (DEPTH, D_MODEL, IN_W), D_MODEL ** -0.5),
        'b_gate': nrm(3, (DEPTH, N_BRANCH, D_MODEL), 0.02),
        'lambda_q1': nrm(4, (DEPTH, ATT_DK), 0.1),
        'lambda_k1': nrm(5, (DEPTH, ATT_DK), 0.1),
        'lambda_q2': nrm(6, (DEPTH, ATT_DK), 0.1),
        'lambda_k2': nrm(7, (DEPTH, ATT_DK), 0.1),
        'att_norm_g': 1.0 + nrm(8, (DEPTH, ATT_DV), 0.02),
        'sgu_ln_g': 1.0 + nrm(9, (DEPTH, SGU_W), 0.02),
        'sgu_ln_b': nrm(10, (DEPTH, SGU_W), 0.02),
        'sgu_w_s': nrm(11, (DEPTH, SGU_GROUPS, SGU_CHUNK, SGU_CHUNK), SGU_CHUNK ** -0.5),
        'sgu_b_s': 1.0 + nrm(12, (DEPTH, SGU_GROUPS, SGU_CHUNK), 0.02),
        's5_a_re': -0.5 + nrm(13, (DEPTH, 2, S5_GROUPS, S5_N), 0.01),
        's5_a_im': a_im_base + nrm(14, (DEPTH, 2, S5_GROUPS, S5_N), 0.01),
        's5_log_step': jax.random.uniform(ks[15], (DEPTH, 2, S5_GROUPS), f32,
                                          minval=math.log(DT_MIN), maxval=math.log(DT_MAX)),
        's5_b_re': nrm(16, (DEPTH, 2, S5_GROUPS, S5_N, S5_GC), (2 * S5_GC) ** -0.5),
        's5_b_im': nrm(17, (DEPTH, 2, S5_GROUPS, S5_N, S5_GC), (2 * S5_GC) ** -0.5),
        's5_c_re': nrm(18, (DEPTH, 2, S5_GROUPS, S5_GC, S5_N), S5_N ** -0.5),
        's5_c_im': nrm(19, (DEPTH, 2, S5_GROUPS, S5_GC, S5_N), S5_N ** -0.5),
        's5_d': nrm(20, (DEPTH, S5_W), 1.0),
        's5_glu_w': nrm(21, (DEPTH, S5_W, S5_W), S5_W ** -0.5),
        's5_glu_b': nrm(22, (DEPTH, S5_W), 0.02),
        'w_branch': nrm(23, (DEPTH, N_BRANCH, ATT_W, D_MODEL), ATT_W ** -0.5),
        'w_o': nrm(24, (DEPTH, D_MODEL, D_MODEL), BETA * D_MODEL ** -0.5),
        'ln1_g': 1.0 + nrm(25, (DEPTH, D_MODEL), 0.02),
        'ln1_b': nrm(26, (DEPTH, D_MODEL), 0.02),
        'peer_w_q': nrm(27, (DEPTH, D_MODEL, PEER_HEADS * PEER_DQ), D_MODEL ** -0.5),
        'peer_keys': nrm(28, (DEPTH, PEER_HEADS, 2, PEER_NKEYS, PEER_DHALF), PEER_DHALF ** -0.5),
        'peer_u': nrm(29, (DEPTH, PEER_EXPERTS, D_MODEL), D_MODEL ** -0.5),
        'peer_v': nrm(30, (DEPTH, PEER_EXPERTS, D_MODEL), BETA * PEER_HEADS ** -0.5),
        'ln2_g': 1.0 + nrm(31, (DEPTH, D_MODEL), 0.02),
        'ln2_b': nrm(32, (DEPTH, D_MODEL), 0.02),
    }


def reference(x_prompt, x_sample, w_in, b_gate, lambda_q1, lambda_k1, lambda_q2, lambda_k2, att_norm_g,
              sgu_ln_g, sgu_ln_b, sgu_w_s, sgu_b_s,
              s5_a_re, s5_a_im, s5_log_step, s5_b_re, s5_b_im, s5_c_re, s5_c_im, s5_d, s5_glu_w, s5_glu_b,
              w_branch, w_o, ln1_g, ln1_b, peer_w_q, peer_keys, peer_u, peer_v, ln2_g, ln2_b):
    def trunk(x):
        for l in range(DEPTH):
            lam_init = 0.8 - 0.6 * math.exp(-0.3 * l)
            x = encoder_layer(x, lam_init, w_in[l], b_gate[l], lambda_q1[l], lambda_k1[l], lambda_q2[l],
                              lambda_k2[l], att_norm_g[l], sgu_ln_g[l], sgu_ln_b[l], sgu_w_s[l], sgu_b_s[l],
                              s5_a_re[l], s5_a_im[l], s5_log_step[l], s5_b_re[l], s5_b_im[l], s5_c_re[l],
                              s5_c_im[l], s5_d[l], s5_glu_w[l], s5_glu_b[l], w_branch[l], w_o[l],
                              ln1_g[l], ln1_b[l], peer_w_q[l], peer_keys[l], peer_u[l], peer_v[l],
                              ln2_g[l], ln2_b[l])
        return x

    y_prompt = trunk(x_prompt)
    y_sample = trunk(x_sample)
    return (y_prompt, y_sample)
```

```python
import math
from contextlib import ExitStack
import numpy as np
import ml_dtypes
import concourse.bass as bass
import concourse.mybir as mybir
from concourse.bass_utils import run_bass_kernel_spmd

F32 = mybir.dt.float32
BF16 = mybir.dt.bfloat16
U32 = mybir.dt.uint32
I32 = mybir.dt.int32
AF = mybir.ActivationFunctionType
ALU = mybir.AluOpType
AX = mybir.AxisListType

D = 1024
DEPTH = 4
NH = 4
LN_EPS = 1e-5
ALPHA = (2 * DEPTH) ** 0.25
SLOPES = [2.0 ** (-8.0 * (h + 1) / NH) for h in range(NH)]
SEG_OFF = 16384.0
S5T = 512
NEG = -1.0e30


class Buf:
    __slots__ = ("w", "rs")

    def __init__(self):
        self.w = None
        self.rs = {}


class T:
    def __init__(self, t, excl=False):
        self.t = t
        self.b = Buf()
        self.excl = excl

    def __getitem__(self, k):
        return self.t[k]


def _b(x):
    return x.b if isinstance(x, T) else x


class Bld:
    def __init__(self):
        nc = bass.Bass("TRN2", target_bir_lowering=False)
        self.nc = nc
        self.es = ExitStack()
        self.eng = {"pe": nc.tensor, "act": nc.scalar, "dve": nc.vector, "pool": nc.gpsimd, "sp": nc.sync}
        self.sem = {}
        self.cnt = {}
        for e in ("pe", "act", "dve", "pool"):
            self.sem[e] = self.es.enter_context(nc.semaphore("s_" + e))
            self.cnt[e] = 0
        self.seen = {e: {} for e in self.eng}
        self.dkeys = {"sp": [], "pool": []}
        for q, n in (("sp", 48), ("pool", 24)):
            for i in range(n):
                k = ("d", q, i)
                self.sem[k] = self.es.enter_context(nc.semaphore("sd_%s%d" % (q, i)))
                self.cnt[k] = 0
                self.dkeys[q].append(k)
        self.drr = {"sp": 0, "pool": 0}
        self.dreg = {}
        self.nid = 0

    def sb(self, ctx, shape, dt, name=None):
        self.nid += 1
        return T(ctx.enter_context(self.nc.sbuf_tensor("%s_%d" % (name or "sb", self.nid), list(shape), dt)))

    def ps(self, ctx, shape, dt, name=None):
        self.nid += 1
        return T(ctx.enter_context(self.nc.psum_tensor("%s_%d" % (name or "ps", self.nid), list(shape), dt)), excl=True)

    def dr(self, name, key):
        k = (name, key)
        b = self.dreg.get(k)
        if b is None:
            b = self.dreg[k] = Buf()
        return b

    def _deps(self, R, W):
        deps = []
        for r in R:
            b = _b(r)
            if b.w:
                deps.append(b.w)
            if isinstance(r, T) and r.excl:
                deps.extend(b.rs.items())
        for w in W:
            b = _b(w)
            if b.w:
                deps.append(b.w)
            deps.extend(b.rs.items())
        return deps

    def _wait(self, E, deps):
        seen = self.seen[E]
        for k, v in deps:
            if k == E and E == "pe":
                continue
            if seen.get(k, 0) >= v:
                continue
            self.eng[E].wait_ge(self.sem[k], v)
            seen[k] = v

    def _mark(self, tok, R, W):
        k, v = tok
        for r in R:
            b = _b(r)
            if b.rs.get(k, 0) < v:
                b.rs[k] = v
        for w in W:
            b = _b(w)
            b.w = tok
            b.rs = {}

    def op(self, E, fn, R=(), W=()):
        self._wait(E, self._deps(R, W))
        ins = fn(self.eng[E])
        self.cnt[E] += 1
        ins.then_inc(self.sem[E], 1)
        tok = (E, self.cnt[E])
        self._mark(tok, R, W)
        return tok

    def dma(self, out, in_, R=(), W=(), Q="sp", ind=None, bc=None):
        self._wait(Q, self._deps(R, W))
        i = self.drr[Q]
        self.drr[Q] = (i + 1) % len(self.dkeys[Q])
        key = self.dkeys[Q][i]
        if ind is None:
            ins = self.eng[Q].dma_start(out=out, in_=in_)
        else:
            ins = self.eng[Q].indirect_dma_start(out=out, out_offset=None, in_=in_, in_offset=ind, bounds_check=bc, oob_is_err=False)
        self.cnt[key] += 16
        ins.then_inc(self.sem[key], 16)
        tok = (key, self.cnt[key])
        self._mark(tok, R, W)
        return tok

    def barrier(self):
        allk = [(k, v) for k, v in self.cnt.items() if v > 0]
        for E in self.eng:
            self._wait(E, allk)


SKIP = set()
DBG = {"fm": True, "v": True, "zv": True, "ln": True, "sgu": True}


def build(Lc, depth=DEPTH, debug=False, stop_after=None, nexp=16384):
    B = Bld()
    nc = B.nc
    NT = Lc // 128
    NCH = Lc // 512
    HALF_T = NT // 2
    kind_s = "ExternalOutput" if debug else "Internal"

    def din(name, shape, dt=F32):
        return nc.dram_tensor(name, list(shape), dt, kind="ExternalInput").ap()

    def dsc(name, shape, dt=F32):
        return nc.dram_tensor(name, list(shape), dt, kind=kind_s).ap()

    x_in = din("x", [Lc, D])
    gate_in = din("gate", [128, 1])
    eoff_in = din("eoff", [128, 1])
    qaug_in = din("qaug", [NH, 6, Lc], BF16)
    kaug_in = din("kaug", [NH, 2, 6, Lc], BF16)
    negb_in = din("negb", [128, NH, 128])
    bmask_in = din("bmask", [128, 2])
    w_in = din("w_in", [DEPTH, D, 6144])
    b_gate = din("b_gate", [DEPTH, 128, 3, 8])
    lam_in = din("lam4", [DEPTH, 4, 64])
    att_norm_g = din("att_norm_g", [DEPTH, 128])
    sgu_ln_g = din("sgu_ln_g", [DEPTH, 512])
    sgu_ln_b = din("sgu_ln_b", [DEPTH, 512])
    sgu_w_s = din("sgu_w_s", [DEPTH, 4, 128, 128])
    sgu_b_s = din("sgu_b_s", [DEPTH, 512])
    s5p_in = din("s5p", [DEPTH, 128, 3, 32])
    s5q_in = din("s5q", [DEPTH, 128, 3, 512])
    s5b_in = din("s5b", [DEPTH, 128, 2, 512])
    s5c_in = din("s5c", [DEPTH, 128, 2, 32, 32])
    s5_d = din("s5_d", [DEPTH, 128, 4])
    s5_glu_w = din("s5_glu_w", [DEPTH, 512, 512])
    s5_glu_b = din("s5_glu_b", [DEPTH, 128, 4])
    w_branch = din("w_branch", [DEPTH, 3, 512, D])
    w_o = din("w_o", [DEPTH, D, D])
    ln1_g = din("ln1_g", [DEPTH, D])
    ln1_b = din("ln1_b", [DEPTH, D])
    peer_w_q = din("peer_w_q", [DEPTH, D, 2048])
    peer_keys = din("peer_keys", [DEPTH, 16, 128, 128])
    peer_uv = din("peer_uv", [DEPTH, nexp, 2 * D])
    ln2_g = din("ln2_g", [DEPTH, D])
    ln2_b = din("ln2_b", [DEPTH, D])
    y_out = nc.dram_tensor("y", [Lc, D], F32, kind="ExternalOutput").ap()

    xT_d = dsc("xT", [8, 128, Lc], BF16)
    x1_d = dsc("x1", [Lc, D])
    x1T_d = dsc("x1T", [8, 128, Lc], BF16)
    x2_d = dsc("x2", [Lc, D])
    qT_d = dsc("qT", [NH, 2, 64, Lc], BF16)
    kT_d = dsc("kT", [NH, 2, 64, Lc], BF16)
    va_d = dsc("vaug", [NT, 128, NH, 129], BF16)
    usb_d = dsc("usb", [4, 128, Lc], BF16)
    usf_d = dsc("usf", [4, 128, Lc])
    attT_d = dsc("attT", [4, 128, Lc], BF16)
    sguT_d = dsc("sguT", [4, 128, Lc], BF16)
    yT_d = dsc("yT", [16, 32, Lc])
    s5T_d = dsc("s5T", [4, 128, Lc], BF16)

    glob = B.es
    ident = B.sb(glob, [128, 128], F32, "ident")
    identb = B.sb(glob, [128, 128], BF16, "identb")
    iot = B.sb(glob, [128, 128], F32, "iot")
    B.op("pool", lambda e: e.iota(iot[:], pattern=[[1, 128]], base=0, channel_multiplier=-1,
                                  allow_small_or_imprecise_dtypes=True), W=[iot])
    B.op("dve", lambda e: e.tensor_scalar(ident[:], iot[:], 0.0, None, op0=ALU.is_equal), R=[iot], W=[ident])
    B.op("dve", lambda e: e.tensor_copy(identb[:], ident[:]), R=[ident], W=[identb])
    gate = B.sb(glob, [128, 1], F32, "gate")
    B.dma(gate[:], gate_in[:, :], W=[gate])
    eoff = B.sb(glob, [128, 1], F32, "eoff")
    B.dma(eoff[:], eoff_in[:, :], W=[eoff])
    eps_t = B.sb(glob, [128, 1], F32, "eps")
    B.op("dve", lambda e: e.memset(eps_t[:], LN_EPS), W=[eps_t])

    def layernorm(ph, xt, g_bc, b_bc, out, width, scr):
        nchk = width // 512
        st = scr["st"]
        mv = scr["mv"]
        rs = scr["rs"]
        for i in range(nchk):
            B.op("dve", lambda e, i=i: e.bn_stats(st[:, i, :], xt[:, i * 512:(i + 1) * 512]), R=[xt], W=[st])
        B.op("dve", lambda e: e.bn_aggr(mv[:], st[:, 0:nchk, :]), R=[st], W=[mv])
        B.op("act", lambda e: e.activation(rs[:], mv[:, 1:2], AF.Sqrt, bias=eps_t[:, 0:1], scale=1.0), R=[mv, eps_t], W=[rs])
        B.op("dve", lambda e: e.reciprocal(rs[:], rs[:]), R=[rs], W=[rs])
        B.op("dve", lambda e: e.tensor_scalar(xt[:, 0:width], xt[:, 0:width], mv[:, 0:1], rs[:, 0:1],
                                              op0=ALU.subtract, op1=ALU.mult), R=[xt, mv, rs], W=[xt])
        B.op("dve", lambda e: e.tensor_tensor(xt[:, 0:width], xt[:, 0:width], g_bc[:, 0:width], op=ALU.mult), R=[xt, g_bc], W=[xt])
        B.op("dve", lambda e: e.tensor_tensor(out[:, 0:width], xt[:, 0:width], b_bc[:, 0:width], op=ALU.add), R=[xt, b_bc], W=[out])

    def ln_scratch(ph):
        return {"st": B.sb(ph, [128, 2, 6], F32, "lnst"), "mv": B.sb(ph, [128, 2], F32, "lnmv"),
                "rs": B.sb(ph, [128, 1], F32, "lnrs")}

    def load_w_bf16(ph, dst, src_rows, ncols, stage, col0=0, kcs=8):
        step = stage[0].t.shape[1]
        n = 0
        for kc in range(kcs):
            for c0 in range(0, ncols, step):
                cw = min(step, ncols - c0)
                stg = stage[n % len(stage)]
                B.dma(stg[:, 0:cw], src_rows(kc)[:, c0:c0 + cw], W=[stg])
                eng = "pool" if n % 2 == 0 else "dve"
                B.op(eng, lambda e, kc=kc, c0=c0, cw=cw, stg=stg: e.tensor_copy(dst[:, kc, col0 + c0:col0 + c0 + cw], stg[:, 0:cw]),
                     R=[stg], W=[dst])
                n += 1

    def to_fm(ph, src, psT, dstc, j, nk=8):
        for k in range(nk):
            B.op("pe", lambda e, k=k: e.transpose(psT[:, k * 128:(k + 1) * 128], src[:, k * 128:(k + 1) * 128], ident[:]),
                 R=[src, ident], W=[psT])
        B.op("act", lambda e: e.copy(dstc[:, 0:nk, j * 128:(j + 1) * 128], psT[:, 0:nk * 128].rearrange("p (k t) -> p k t", k=nk)),
             R=[psT], W=[dstc])

    def phase_x0():
        with ExitStack() as ph:
            xin = [B.sb(ph, [128, D], F32, "xin") for _ in range(2)]
            xc = [B.sb(ph, [128, 8, 512], BF16, "xc") for _ in range(2)]
            psT = [B.ps(ph, [128, 1024], F32, "psT") for _ in range(2)]
            for c in range(NCH):
                cb = xc[c % 2]
                for j in range(4):
                    t = c * 4 + j
                    xi = xin[t % 2]
                    B.dma(xi[:], x_in[t * 128:(t + 1) * 128, :], W=[xi])
                    to_fm(ph, xi, psT[t % 2], cb, j)
                B.dma(xT_d[:, :, c * 512:(c + 1) * 512].rearrange("k p t -> p k t"), cb[:], R=[cb],
                      W=[B.dr("xT", c)])
            B.barrier()

    def phase_inproj(l):
        with ExitStack() as ph:
            W = B.sb(ph, [128, 8, 3072], BF16, "W")
            stage = [B.sb(ph, [128, 1536], F32, "stg") for _ in range(2)]
            load_w_bf16(ph, W, lambda kc: w_in[l, kc * 128:(kc + 1) * 128, 0:3072], 3072, stage)
            lng = B.sb(ph, [128, 512], F32, "lng")
            lnb = B.sb(ph, [128, 512], F32, "lnb")
            bsb = B.sb(ph, [128, 512], F32, "bsb")
            B.dma(lng[:], sgu_ln_g[l:l + 1, :].to_broadcast([128, 512]), W=[lng])
            B.dma(lnb[:], sgu_ln_b[l:l + 1, :].to_broadcast([128, 512]), W=[lnb])
            B.dma(bsb[:], sgu_b_s[l:l + 1, :].to_broadcast([128, 512]), W=[bsb])
            wsf = B.sb(ph, [128, 4, 128], F32, "wsf")
            wsT = B.sb(ph, [128, 4, 128], BF16, "wsT")
            B.dma(wsf[:], sgu_w_s[l].rearrange("g t s -> t g s"), W=[wsf])
            psT = B.ps(ph, [128, 512], F32, "psT")
            for g in range(4):
                B.op("pe", lambda e, g=g: e.transpose(psT[:, g * 128:(g + 1) * 128], wsf[:, g, :], ident[:]), R=[wsf, ident], W=[psT])
            B.op("act", lambda e: e.copy(wsT[:].rearrange("p g t -> p (g t)"), psT[:]), R=[psT], W=[wsT])
            lns = ln_scratch(ph)
            pfm = [B.ps(ph, [128, 512], F32, "pfm") for _ in range(3)]
            ptm = [B.ps(ph, [128, 512], F32, "ptm") for _ in range(2)]
            psg = B.ps(ph, [128, 512], F32, "psg")
            xc = [B.sb(ph, [128, 8, 512], BF16, "xc") for _ in range(2)]
            qk = [B.sb(ph, [128, 512], BF16, "qk") for _ in range(3)]
            uT = [B.sb(ph, [128, 4, 512], F32, "uT") for _ in range(2)]
            usf = [B.sb(ph, [128, 512], F32, "usf") for _ in range(2)]
            usb = [B.sb(ph, [128, 512], BF16, "usb") for _ in range(2)]
            vau = [B.sb(ph, [128, NH, 129], BF16, "vau") for _ in range(2)]
            for v in vau:
                B.op("pool", lambda e, v=v: e.memset(v[:], 1.0), W=[v])
            zv = [B.sb(ph, [128, 512], F32, "zv") for _ in range(2)]
            vln = [B.sb(ph, [128, 512], BF16, "vln") for _ in range(2)]
            svt = [B.sb(ph, [128, 512], F32, "svt") for _ in range(2)]
            sgc = [B.sb(ph, [128, 4, 512], BF16, "sgc") for _ in range(2)]
            nfm = 0
            ntm = 0
            for c in range(NCH):
                cs = slice(c * 512, (c + 1) * 512)
                xcc = xc[c % 2]
                B.dma(xcc[:], xT_d[:, :, cs].rearrange("k p t -> p k t"), R=[B.dr("xT", c)], W=[xcc])
                uTc = uT[c % 2]
                for blk in range(24 if DBG["fm"] else 0):
                    if 8 <= blk < 12:
                        continue
                    if 16 <= blk < 20:
                        continue
                    col0 = blk * 128
                    pp = pfm[nfm % 3]
                    for kc in range(8):
                        B.op("pe", lambda e, kc=kc, pp=pp, col0=col0: e.matmul(pp[:], lhsT=W[:, kc, col0:col0 + 128], rhs=xcc[:, kc, :],
                                                                               start=(kc == 0), stop=(kc == 7)), R=[W, xcc], W=[pp])
                    if blk < 8:
                        o = qk[nfm % 3]
                        sc = 0.125 if blk < 4 else 1.0
                        B.op("act", lambda e, o=o, pp=pp, sc=sc: e.activation(o[:], pp[:], AF.Copy, scale=sc), R=[pp], W=[o])
                        dst = qT_d if blk < 4 else kT_d
                        h = blk % 4
                        B.dma(dst[h, :, :, cs].rearrange("c d t -> (c d) t"), o[:], R=[o], W=[B.dr("qk%d" % (blk // 4), (h, c))])
                    elif blk < 16:
                        g = blk - 12
                        B.op("act", lambda e, g=g, pp=pp: e.activation(uTc[:, g, :], pp[:], AF.Gelu), R=[pp], W=[uTc])
                    else:
                        g = blk - 20
                        of = usf[g % 2]
                        ob = usb[g % 2]
                        B.op("act", lambda e, of=of, pp=pp: e.copy(of[:], pp[:]), R=[pp], W=[of])
                        B.op("dve", lambda e, ob=ob, of=of: e.tensor_copy(ob[:], of[:]), R=[of], W=[ob])
                        B.dma(usf_d[g, :, cs], of[:], R=[of], W=[B.dr("usf", (g, c))])
                        B.dma(usb_d[g, :, cs], ob[:], R=[ob], W=[B.dr("usb", (g, c))])
                    nfm += 1
                sg = sgc[c % 2]
                for j in range(4 if DBG["v"] else 0):
                    t = c * 4 + j
                    ts = slice(j * 128, (j + 1) * 128)
                    pp = ptm[ntm % 2]
                    ntm += 1
                    for kc in range(8):
                        B.op("pe", lambda e, kc=kc, pp=pp: e.matmul(pp[:], lhsT=xcc[:, kc, ts], rhs=W[:, kc, 1024:1536],
                                                                    start=(kc == 0), stop=(kc == 7)), R=[W, xcc], W=[pp])
                    va = vau[t % 2]
                    B.op("act", lambda e, va=va, pp=pp: e.copy(va[:, :, 0:128], pp[:].rearrange("p (h e) -> p h e", h=NH)), R=[pp], W=[va])
                    B.dma(va_d[t], va[:], R=[va], W=[B.dr("va", t)])
                    if not DBG["zv"]:
                        continue
                    pp = ptm[ntm % 2]
                    ntm += 1
                    for kc in range(8):
                        B.op("pe", lambda e, kc=kc, pp=pp: e.matmul(pp[:], lhsT=xcc[:, kc, ts], rhs=W[:, kc, 2048:2560],
                                                                    start=(kc == 0), stop=(kc == 7)), R=[W, xcc], W=[pp])
                    z = zv[t % 2]
                    B.op("act", lambda e, z=z, pp=pp: e.activation(z[:], pp[:], AF.Gelu), R=[pp], W=[z])
                    vl = vln[t % 2]
                    if DBG["ln"]:
                        layernorm(ph, z, lng, lnb, z, 512, lns)
                    if not DBG["sgu"]:
                        continue
                    B.op("pool", lambda e, vl=vl, z=z: e.tensor_copy(vl[:], z[:]), R=[z], W=[vl])
                    for g in range(4):
                        B.op("pe", lambda e, g=g, vl=vl: e.matmul(psg[:, g * 128:(g + 1) * 128], lhsT=vl[:, g * 128:(g + 1) * 128],
                                                                  rhs=wsT[:, g, :], start=True, stop=True), R=[vl, wsT], W=[psg])
                    sv = svt[t % 2]
                    B.op("dve", lambda e, sv=sv: e.tensor_tensor(sv[:], psg[:], bsb[:], op=ALU.add), R=[psg, bsb], W=[sv])
                    B.op("dve", lambda e, sv=sv, sg=sg, ts=ts: e.tensor_tensor(sg[:, :, ts], sv[:].rearrange("p (g t) -> p g t", g=4),
                                                                                 uTc[:, :, ts], op=ALU.mult), R=[sv, uTc], W=[sg])
                if DBG["sgu"] and DBG["zv"] and DBG["v"]:
                    B.dma(sguT_d[:, :, cs].rearrange("g p t -> p g t"), sg[:], R=[sg], W=[B.dr("sguT", c)])
            B.barrier()

    def skip_blk(h, kb, qc):
        if qc * 4 <= kb < qc * 4 + 4:
            return False
        if kb < qc * 4:
            d = (qc * 4 - kb - 1) * 128 + 1
        else:
            d = (kb - qc * 4 - 4) * 128 + 1
        return SLOPES[h] * d > 45.0

    def phase_attn(l):
        lam_init = 0.8 - 0.6 * math.exp(-0.3 * l)
        with ExitStack() as ph:
            lt = B.sb(ph, [128, 4, 64], F32, "lt")
            B.dma(lt[:], lam_in[l:l + 1].rearrange("o a d -> o (a d)").to_broadcast([128, 256]).rearrange("p (a d) -> p a d", a=4), W=[lt])
            lj = B.sb(ph, [128, 2, 64], F32, "lj")
            ls2 = B.sb(ph, [128, 2], F32, "ls2")
            B.op("dve", lambda e: e.tensor_tensor(lj[:, 0, :], lt[:, 0, :], lt[:, 1, :], op=ALU.mult), R=[lt], W=[lj])
            B.op("dve", lambda e: e.tensor_tensor(lj[:, 1, :], lt[:, 2, :], lt[:, 3, :], op=ALU.mult), R=[lt, lj], W=[lj])
            B.op("dve", lambda e: e.tensor_reduce(ls2[:], lj[:], axis=AX.X, op=ALU.add), R=[lj], W=[ls2])
            B.op("act", lambda e: e.activation(ls2[:], ls2[:], AF.Exp), R=[ls2], W=[ls2])
            neglam = B.sb(ph, [128, 1], F32, "neglam")
            B.op("dve", lambda e: e.scalar_tensor_tensor(neglam[:], ls2[:, 1:2], -lam_init, ls2[:, 0:1], op0=ALU.add, op1=ALU.subtract),
                 R=[ls2], W=[neglam])
            gv = B.sb(ph, [128, 128], F32, "gv")
            B.dma(gv[:], att_norm_g[l:l + 1, :].to_broadcast([128, 128]), W=[gv])
            B.op("dve", lambda e: e.tensor_scalar(gv[:], gv[:], 1.0 - lam_init, None, op0=ALU.mult), R=[gv], W=[gv])
            negb = B.sb(ph, [128, NH, 128], F32, "negb")
            B.dma(negb[:], negb_in[:, :, :], W=[negb])
            KT = [[B.sb(ph, [70, Lc], BF16, "KT") for _ in range(2)] for _ in range(2)]
            V = B.sb(ph, [128, NT, 129], BF16, "V")
            Qc = [B.sb(ph, [70, 512], BF16, "Qc") for _ in range(4)]
            Pt = [B.sb(ph, [128, 512], BF16, "Pt") for _ in range(3)]
            pss = [B.ps(ph, [128, 512], F32, "pss") for _ in range(2)]
            pacc = [B.ps(ph, [128, 512], F32, "pacc") for _ in range(4)]
            psT = B.ps(ph, [128, 512], BF16, "psTb")
            o = [B.sb(ph, [128, 4, 128], F32, "o") for _ in range(2)]
            rz = B.sb(ph, [128, 1], F32, "rz")
            att = B.sb(ph, [128, 4, 128], F32, "att")
            sq = B.sb(ph, [128, 4, 128], F32, "sq")
            ss = B.sb(ph, [128, 4], F32, "ss")
            attb = B.sb(ph, [128, 4, 128], BF16, "attb")
            attT = [B.sb(ph, [128, 512], BF16, "attT") for _ in range(2)]
            nq = 0
            nsc = 0
            for h in range(NH):
                for comp in range(2):
                    for ver in range(2):
                        kt = KT[comp][ver]
                        B.dma(kt[0:64, :], kT_d[h, comp, :, :], R=[B.dr("qk1", (h, c)) for c in range(NCH)], W=[kt])
                        B.dma(kt[64:70, :], kaug_in[h, ver, :, :], W=[kt])
                B.dma(V[:], va_d[:, :, h, :].rearrange("n p e -> p n e"), R=[B.dr("va", t) for t in range(NT)], W=[V])
                for qc in range(NCH):
                    cs = slice(qc * 512, (qc + 1) * 512)
                    for comp in range(2):
                        q = Qc[nq % 4]
                        nq += 1
                        B.dma(q[0:64, :], qT_d[h, comp, :, cs], R=[B.dr("qk0", (h, qc))], W=[q])
                        B.dma(q[64:70, :], qaug_in[h, :, cs], W=[q])
                        kbs = [kb for kb in range(NT) if not skip_blk(h, kb, qc)]
                        def emit_score(idx, q=q, comp=comp, kbs=kbs, qc=qc, h=h, base=nsc):
                            kb = kbs[idx]
                            ks = slice(kb * 128, (kb + 1) * 128)
                            ps_ = pss[(base + idx) % 2]
                            if kb // 4 == qc:
                                for j in range(4):
                                    qb = qc * 4 + j
                                    js = slice(j * 128, (j + 1) * 128)
                                    if kb == qb:
                                        kt = KT[comp][0]
                                        B.op("pe", lambda e, kt=kt, ps_=ps_, js=js, ks=ks, q=q: e.matmul(ps_[:, js], lhsT=kt[0:64, ks], rhs=q[0:64, js],
                                                                                                       start=True, stop=True), R=[kt, q], W=[ps_])
                                    else:
                                        kt = KT[comp][0 if kb < qb else 1]
                                        B.op("pe", lambda e, kt=kt, ps_=ps_, js=js, ks=ks, q=q: e.matmul(ps_[:, js], lhsT=kt[0:70, ks], rhs=q[0:70, js],
                                                                                                       start=True, stop=True), R=[kt, q], W=[ps_])
                                jd = kb - qc * 4
                                jds = slice(jd * 128, (jd + 1) * 128)
                                B.op("dve", lambda e, ps_=ps_, jds=jds, h=h: e.tensor_tensor(ps_[:, jds], ps_[:, jds], negb[:, h, :], op=ALU.add),
                                     R=[ps_, negb], W=[ps_])
                            else:
                                kt = KT[comp][0 if kb < qc * 4 else 1]
                                B.op("pe", lambda e, kt=kt, ps_=ps_, ks=ks, q=q: e.matmul(ps_[:], lhsT=kt[0:70, ks], rhs=q[0:70, :],
                                                                                        start=True, stop=True), R=[kt, q], W=[ps_])

                        def emit_rest(idx, kbs=kbs, base=nsc):
                            kb = kbs[idx]
                            ps_ = pss[(base + idx) % 2]
                            P = Pt[(base + idx) % 3]
                            B.op("act", lambda e, P=P, ps_=ps_: e.activation(P[:], ps_[:], AF.Exp), R=[ps_], W=[P])
                            for j in range(4):
                                js = slice(j * 128, (j + 1) * 128)
                                B.op("pe", lambda e, j=j, js=js, P=P, kb=kb, idx=idx: e.matmul(pacc[j][:, 0:129], lhsT=P[:, js], rhs=V[:, kb, :],
                                                                                              start=(idx == 0), stop=(idx == len(kbs) - 1)),
                                     R=[P, V], W=[pacc[j]])

                        emit_score(0)
                        for idx in range(len(kbs)):
                            if idx + 1 < len(kbs):
                                emit_score(idx + 1)
                            emit_rest(idx)
                        nsc += len(kbs)
                        oc = o[comp]
                        for j in range(4):
                            B.op("dve", lambda e, j=j: e.reciprocal(rz[:], pacc[j][:, 128:129]), R=[pacc[j]], W=[rz])
                            B.op("dve", lambda e, j=j, oc=oc: e.tensor_scalar(oc[:, j, :], pacc[j][:, 0:128], rz[:, 0:1], None, op0=ALU.mult),
                                 R=[pacc[j], rz], W=[oc])
                    fl = lambda t_: t_[:].rearrange("p j e -> p (j e)")
                    B.op("dve", lambda e: e.scalar_tensor_tensor(fl(att), fl(o[1]), neglam[:, 0:1], fl(o[0]), op0=ALU.mult, op1=ALU.add),
                         R=[o[0], o[1], neglam], W=[att])
                    B.op("pool", lambda e: e.tensor_tensor(fl(sq), fl(att), fl(att), op=ALU.mult), R=[att], W=[sq])
                    B.op("dve", lambda e: e.tensor_reduce(ss[:], sq[:], axis=AX.X, op=ALU.add), R=[sq], W=[ss])
                    B.op("act", lambda e: e.activation(ss[:], ss[:], AF.Sqrt, bias=eps_t[:, 0:1], scale=1.0 / 128.0), R=[ss, eps_t], W=[ss])
                    B.op("dve", lambda e: e.reciprocal(ss[:], ss[:]), R=[ss], W=[ss])
                    B.op("dve", lambda e: e.tensor_tensor(att[:], att[:], ss[:].unsqueeze(2).to_broadcast([128, 4, 128]), op=ALU.mult),
                         R=[att, ss], W=[att])
                    B.op("dve", lambda e: e.tensor_tensor(attb[:], att[:], gv[:].unsqueeze(1).to_broadcast([128, 4, 128]), op=ALU.mult),
                         R=[att, gv], W=[attb])
                    for j in range(4):
                        B.op("pe", lambda e, j=j: e.transpose(psT[:, j * 128:(j + 1) * 128], attb[:, j, :], identb[:]), R=[attb, identb], W=[psT])
                    at = attT[qc % 2]
                    B.op("act", lambda e, at=at: e.copy(at[:], psT[:]), R=[psT], W=[at])
                    B.dma(attT_d[h, :, cs], at[:], R=[at], W=[B.dr("attT", (h, qc))])
            B.barrier()

    TWO_PI = 2.0 * math.pi
    SC_S = TWO_PI * (1.0 - 1e-6)
    SC_B = -math.pi * (1.0 - 1e-6)

    def sincos(a, n, sout, cout, scr):
        r, ri, rf, fx = scr[0:4]
        for off, (ot, oap) in ((0.5, sout), (0.75, cout)):
            B.op("dve", lambda e, off=off: e.tensor_scalar(r[:, 0:n], a, 1.0 / TWO_PI, off, op0=ALU.mult, op1=ALU.add), R=scr[4], W=[r])
            B.op("dve", lambda e: e.tensor_copy(ri[:, 0:n], r[:, 0:n]), R=[r], W=[ri])
            B.op("dve", lambda e: e.tensor_copy(rf[:, 0:n], ri[:, 0:n]), R=[ri], W=[rf])
            B.op("dve", lambda e: e.tensor_tensor(fx[:, 0:n], rf[:, 0:n], r[:, 0:n], op=ALU.is_gt), R=[rf, r], W=[fx])
            B.op("dve", lambda e: e.tensor_tensor(rf[:, 0:n], rf[:, 0:n], fx[:, 0:n], op=ALU.subtract), R=[rf, fx], W=[rf])
            B.op("dve", lambda e: e.tensor_tensor(r[:, 0:n], r[:, 0:n], rf[:, 0:n], op=ALU.subtract), R=[r, rf], W=[r])
            B.op("act", lambda e, oap=oap: e.activation(oap, r[:, 0:n], AF.Sin, bias=sinb[:, 0:1], scale=SC_S), R=[r, sinb], W=[ot])

    sinb = B.sb(glob, [128, 1], F32, "sinb")
    B.op("dve", lambda e: e.memset(sinb[:], SC_B), W=[sinb])
    iot_s = B.sb(glob, [128, 512], F32, "iot_s")
    B.op("pool", lambda e: e.iota(iot_s[:], pattern=[[1, 512]], base=0, channel_multiplier=0,
                                  allow_small_or_imprecise_dtypes=True), W=[iot_s])
    bmask = B.sb(glob, [128, 2], F32, "bmask")
    B.dma(bmask[:], bmask_in[:, :], W=[bmask])

    def phase_s5(l):
        with ExitStack() as ph:
            mk = lambda shp, dt=F32, nm="t": B.sb(ph, shp, dt, nm)
            scr_t = [mk([128, 512]), mk([128, 512], I32), mk([128, 512]), mk([128, 512])]
            sp = mk([128, 3, 32])
            B.dma(sp[:], s5p_in[l], W=[sp])
            dt_ = mk([128, 32]); rho = mk([128, 32]); th = mk([128, 32]); thT = mk([128, 32])
            cT = mk([128, 32]); sT = mk([128, 32])
            B.op("act", lambda e: e.activation(dt_[:], sp[:, 2, :], AF.Exp), R=[sp], W=[dt_])
            B.op("dve", lambda e: e.tensor_tensor(rho[:], sp[:, 0, :], dt_[:], op=ALU.mult), R=[sp, dt_], W=[rho])
            B.op("act", lambda e: e.activation(rho[:], rho[:], AF.Exp), R=[rho], W=[rho])
            B.op("dve", lambda e: e.tensor_tensor(th[:], sp[:, 1, :], dt_[:], op=ALU.mult), R=[sp, dt_], W=[th])
            B.op("dve", lambda e: e.tensor_scalar(thT[:], th[:], float(S5T), None, op0=ALU.mult), R=[th], W=[thT])
            sincos(thT[:], 32, (sT, sT[:]), (cT, cT[:]), scr_t + [[thT]])
            sq = mk([128, 3, 512]); bq = mk([128, 2, 512])
            B.dma(sq[:], s5q_in[l], W=[sq])
            B.dma(bq[:], s5b_in[l], W=[bq])
            dq = mk([128, 512]); mag = mk([128, 512]); tq = mk([128, 512]); sn = mk([128, 512]); cs_ = mk([128, 512])
            B.op("act", lambda e: e.activation(dq[:], sq[:, 2, :], AF.Exp), R=[sq], W=[dq])
            B.op("dve", lambda e: e.tensor_tensor(mag[:], sq[:, 0, :], dq[:], op=ALU.mult), R=[sq, dq], W=[mag])
            B.op("act", lambda e: e.activation(mag[:], mag[:], AF.Exp), R=[mag], W=[mag])
            B.op("dve", lambda e: e.tensor_tensor(tq[:], sq[:, 1, :], dq[:], op=ALU.mult), R=[sq, dq], W=[tq])
            sincos(tq[:], 512, (sn, sn[:]), (cs_, cs_[:]), scr_t + [[tq]])
            abr = mk([128, 512]); abi = mk([128, 512]); den = mk([128, 512]); t1 = mk([128, 512]); t2 = mk([128, 512])
            fr = mk([128, 512]); fi = mk([128, 512])
            tt = lambda o_, a_, b_, op_, R_, W_: B.op("dve", lambda e: e.tensor_tensor(o_, a_, b_, op=op_), R=R_, W=W_)
            tt(abr[:], mag[:], cs_[:], ALU.mult, [mag, cs_], [abr])
            tt(abi[:], mag[:], sn[:], ALU.mult, [mag, sn], [abi])
            tt(den[:], sq[:, 0, :], sq[:, 0, :], ALU.mult, [sq], [den])
            tt(t1[:], sq[:, 1, :], sq[:, 1, :], ALU.mult, [sq], [t1])
            tt(den[:], den[:], t1[:], ALU.add, [den, t1], [den])
            B.op("dve", lambda e: e.reciprocal(den[:], den[:]), R=[den], W=[den])
            B.op("dve", lambda e: e.tensor_scalar(abr[:], abr[:], -1.0, None, op0=ALU.add), R=[abr], W=[abr])
            tt(t1[:], abr[:], sq[:, 0, :], ALU.mult, [abr, sq], [t1])
            tt(t2[:], abi[:], sq[:, 1, :], ALU.mult, [abi, sq], [t2])
            tt(fr[:], t1[:], t2[:], ALU.add, [t1, t2], [fr])
            tt(fr[:], fr[:], den[:], ALU.mult, [fr, den], [fr])
            tt(t1[:], abi[:], sq[:, 0, :], ALU.mult, [abi, sq], [t1])
            tt(t2[:], abr[:], sq[:, 1, :], ALU.mult, [abr, sq], [t2])
            tt(fi[:], t1[:], t2[:], ALU.subtract, [t1, t2], [fi])
            tt(fi[:], fi[:], den[:], ALU.mult, [fi, den], [fi])
            bbr = abr; bbi = abi
            tt(t1[:], fr[:], bq[:, 0, :], ALU.mult, [fr, bq], [t1])
            tt(t2[:], fi[:], bq[:, 1, :], ALU.mult, [fi, bq], [t2])
            tt(bbr[:], t1[:], t2[:], ALU.subtract, [t1, t2], [bbr])
            tt(t1[:], fr[:], bq[:, 1, :], ALU.mult, [fr, bq], [t1])
            tt(t2[:], fi[:], bq[:, 0, :], ALU.mult, [fi, bq], [t2])
            tt(bbi[:], t1[:], t2[:], ALU.add, [t1, t2], [bbi])
            BbT = mk([128, 8, 2, 2, 64], BF16)
            for ri, src in ((0, bbr), (1, bbi)):
                for gl in range(2):
                    B.op("dve", lambda e, ri=ri, gl=gl, src=src: e.tensor_scalar(BbT[:, :, ri, gl, :], src[:].rearrange("p (a n) -> p a n", a=8),
                                                                                bmask[:, gl:gl + 1], None, op0=ALU.mult), R=[src, bmask], W=[BbT])
            cf = mk([128, 2, 32, 32]); CT = mk([128, 2, 32, 32], BF16)
            B.dma(cf[:], s5c_in[l], W=[cf])
            B.op("dve", lambda e: e.tensor_copy(CT[:, 0], cf[:, 0]), R=[cf], W=[CT])
            B.op("dve", lambda e: e.tensor_scalar(CT[:, 1], cf[:, 1], -1.0, None, op0=ALU.mult), R=[cf, CT], W=[CT])
            us = [mk([128, Lc], BF16, "us") for _ in range(2)]
            Bls = [mk([32, 2, 2, 128], BF16, "Bl") for _ in range(2)]
            XF = [mk([128, Lc], BF16, "XF") for _ in range(2)]
            tab = [[mk([128, 512]) for _ in range(2)] for _ in range(2)]
            rhob = [mk([128, 512]) for _ in range(2)]
            phs = mk([128, 512])
            pB = [[B.ps(ph, [128, 512], F32, "pB") for _ in range(2)] for _ in range(2)]
            py = [B.ps(ph, [128, 512], F32, "py") for _ in range(2)]
            w1 = [mk([128, 512]) for _ in range(4)]
            bp = [[mk([128, 512]) for _ in range(2)] for _ in range(2)]
            z = [[mk([128, 512]) for _ in range(2)] for _ in range(2)]
            xb = [[mk([128, 512], BF16) for _ in range(2)] for _ in range(2)]
            ini = [[mk([128, 1]) for _ in range(2)] for _ in range(2)]
            i1 = mk([128, 1]); i2 = mk([128, 1])
            yo = [mk([32, 512]) for _ in range(2)]
            n = 0
            for gp in range(16):
                kc = gp // 4
                base = (gp % 4) * 32
                u = us[gp % 2]
                B.dma(u[0:32, :], usb_d[kc, base:base + 32, :], R=[B.dr("usb", (kc, c)) for c in range(NCH)], W=[u])
                Bl = Bls[gp % 2]
                B.dma(Bl[:], BbT[base:base + 32, kc:8:4].rearrange("p d r a n -> p d r (a n)"), R=[BbT], W=[Bl])
                for d in range(2):
                    col = d * 16 + gp
                    B.op("dve", lambda e, col=col: e.tensor_scalar(phs[:], iot_s[:], th[:, col:col + 1], None, op0=ALU.mult), R=[iot_s, th], W=[phs])
                    sincos(phs[:], 512, (tab[d][1], tab[d][1][:]), (tab[d][0], tab[d][0][:]), scr_t + [[phs]])
                    B.op("dve", lambda e, col=col, d=d: e.tensor_scalar(rhob[d][:], iot_s[:], 0.0, rho[:, col:col + 1], op0=ALU.mult, op1=ALU.add),
                         R=[iot_s, rho], W=[rhob[d]])
                for d in range(2):
                    col = d * 16 + gp
                    rev = (d == 1)
                    V_ = (lambda ap: ap[:, ::-1]) if rev else (lambda ap: ap)
                    tc, ts_ = tab[d][0], tab[d][1]
                    order = range(NCH) if d == 0 else range(NCH - 1, -1, -1)
                    prev = None
                    for c in order:
                        cs = slice(c * 512, (c + 1) * 512)
                        pr, pi = pB[n % 2]
                        bpr, bpi = bp[n % 2]
                        zr, zi = z[n % 2]
                        for ri, pp in ((0, pr), (1, pi)):
                            B.op("pe", lambda e, ri=ri, pp=pp, d=d: e.matmul(pp[:], lhsT=Bl[0:32, d, ri, :],
                                                                           rhs=u[0:32, cs], start=True, stop=True), R=[Bl, u], W=[pp])
                        tcv, tsv = V_(tc[:]), V_(ts_[:])
                        tt(w1[0][:], tcv, pr[:], ALU.mult, [tc, pr], [w1[0]])
                        tt(w1[1][:], tsv, pi[:], ALU.mult, [ts_, pi], [w1[1]])
                        tt(w1[2][:], tcv, pi[:], ALU.mult, [tc, pi], [w1[2]])
                        tt(w1[3][:], tsv, pr[:], ALU.mult, [ts_, pr], [w1[3]])
                        B.op("pool", lambda e, bpr=bpr: e.tensor_tensor(bpr[:], w1[0][:], w1[1][:], op=ALU.add), R=[w1[0], w1[1]], W=[bpr])
                        B.op("pool", lambda e, bpi=bpi: e.tensor_tensor(bpi[:], w1[2][:], w1[3][:], op=ALU.subtract), R=[w1[2], w1[3]], W=[bpi])
                        if prev is None:
                            i_r = i_i = 0.0
                            Ri = []
                        else:
                            pzr, pzi = prev
                            lc_ = slice(0, 1) if rev else slice(511, 512)
                            ir, ii = ini[n % 2]
                            B.op("dve", lambda e, pzi=pzi, lc_=lc_: e.tensor_tensor(i1[:], sT[:, col:col + 1], pzi[:, lc_], op=ALU.mult), R=[sT, pzi], W=[i1])
                            B.op("dve", lambda e, pzr=pzr, lc_=lc_: e.tensor_tensor(i2[:], sT[:, col:col + 1], pzr[:, lc_], op=ALU.mult), R=[sT, pzr], W=[i2])
                            B.op("dve", lambda e, pzr=pzr, lc_=lc_, ir=ir: e.scalar_tensor_tensor(ir[:], pzr[:, lc_], cT[:, col:col + 1], i1[:], op0=ALU.mult, op1=ALU.subtract),
                                 R=[cT, pzr, i1], W=[ir])
                            B.op("dve", lambda e, pzi=pzi, lc_=lc_, ii=ii: e.scalar_tensor_tensor(ii[:], pzi[:, lc_], cT[:, col:col + 1], i2[:], op0=ALU.mult, op1=ALU.add),
                                 R=[cT, pzi, i2], W=[ii])
                            bnd = (c == NCH // 2) if d == 0 else (c == NCH // 2 - 1)
                            if bnd:
                                B.op("dve", lambda e, ir=ir: e.tensor_tensor(ir[:], ir[:], gate[:], op=ALU.mult), R=[ir, gate], W=[ir])
                                B.op("dve", lambda e, ii=ii: e.tensor_tensor(ii[:], ii[:], gate[:], op=ALU.mult), R=[ii, gate], W=[ii])
                            i_r, i_i = ir[:, 0:1], ii[:, 0:1]
                            Ri = [ir, ii]
                        rb = rhob[d]
                        B.op("dve", lambda e, zr=zr, bpr=bpr, i_r=i_r: e.tensor_tensor_scan(V_(zr[:]), V_(rb[:]), V_(bpr[:]), i_r, op0=ALU.mult, op1=ALU.add),
                             R=[rb, bpr] + Ri, W=[zr])
                        B.op("dve", lambda e, zi=zi, bpi=bpi, i_i=i_i: e.tensor_tensor_scan(V_(zi[:]), V_(rb[:]), V_(bpi[:]), i_i, op0=ALU.mult, op1=ALU.add),
                             R=[rb, bpi] + Ri, W=[zi])
                        prev = (zr, zi)
                        if d == 0:
                            oxr, oxi = XF[0][:, cs], XF[1][:, cs]
                            Wx = [XF[0]], [XF[1]]
                        else:
                            oxr, oxi = xb[n % 2][0][:], xb[n % 2][1][:]
                            Wx = [xb[n % 2][0]], [xb[n % 2][1]]
                        B.op("pool", lambda e, zr=zr: e.tensor_tensor(w1[0][:], tcv, zr[:], op=ALU.mult), R=[tc, zr], W=[w1[0]])
                        B.op("pool", lambda e, zi=zi: e.tensor_tensor(w1[1][:], tsv, zi[:], op=ALU.mult), R=[ts_, zi], W=[w1[1]])
                        B.op("pool", lambda e, zi=zi: e.tensor_tensor(w1[2][:], tcv, zi[:], op=ALU.mult), R=[tc, zi], W=[w1[2]])
                        B.op("pool", lambda e, zr=zr: e.tensor_tensor(w1[3][:], tsv, zr[:], op=ALU.mult), R=[ts_, zr], W=[w1[3]])
                        tt(oxr, w1[0][:], w1[1][:], ALU.subtract, [w1[0], w1[1]], Wx[0])
                        tt(oxi, w1[2][:], w1[3][:], ALU.add, [w1[2], w1[3]], Wx[1])
                        if d == 1:
                            pyy = py[n % 2]
                            ops = ((0, 0, XF[0], XF[0][:, cs]), (1, 0, XF[1], XF[1][:, cs]), (0, 1, xb[n % 2][0], xb[n % 2][0][:]), (1, 1, xb[n % 2][1], xb[n % 2][1][:]))
                            for k_, (ri, dd, srcT, srcap) in enumerate(ops):
                                B.op("pe", lambda e, ri=ri, dd=dd, srcap=srcap, k_=k_, pyy=pyy: e.matmul(pyy[0:32, :], lhsT=CT[:, ri, dd * 16 + gp, :], rhs=srcap,
                                                                                                      start=(k_ == 0), stop=(k_ == 3)), R=[CT, srcT], W=[pyy])
                            y_ = yo[n % 2]
                            B.op("act", lambda e, y_=y_, pyy=pyy: e.copy(y_[:], pyy[0:32, :]), R=[pyy], W=[y_])
                            B.dma(yT_d[gp, :, cs], y_[:], R=[y_], W=[B.dr("yT", (gp, c))])
                        n += 1
            B.barrier()
        with ExitStack() as ph:
            mk = lambda shp, dt=F32, nm="t": B.sb(ph, shp, dt, nm)
            gw = mk([128, 4, 512], BF16)
            stage = [mk([128, 512]) for _ in range(2)]
            load_w_bf16(ph, gw, lambda kc: s5_glu_w[l, kc * 128:(kc + 1) * 128, :], 512, stage, kcs=4)
            glub = mk([128, 4]); dsk = mk([128, 4])
            B.dma(glub[:], s5_glu_b[l], W=[glub])
            B.dma(dsk[:], s5_d[l], W=[dsk])
            yt = [mk([128, 4, 512]) for _ in range(2)]
            uf = [mk([128, 4, 512]) for _ in range(2)]
            y2f = [mk([128, 4, 512]) for _ in range(2)]
            y2b = [mk([128, 4, 512], BF16) for _ in range(2)]
            sg = [mk([128, 512]) for _ in range(2)]
            so = [mk([128, 4, 512], BF16) for _ in range(2)]
            pg = [B.ps(ph, [128, 512], F32, "pg") for _ in range(2)]
            n = 0
            for c in range(NCH):
                cs = slice(c * 512, (c + 1) * 512)
                y = yt[c % 2]; u = uf[c % 2]; yf = y2f[c % 2]; yb = y2b[c % 2]; s_o = so[c % 2]
                B.dma(y[:], yT_d[:, :, cs].rearrange("(kc q) r t -> (q r) kc t", q=4), R=[B.dr("yT", (gp, c)) for gp in range(16)], W=[y])
                B.dma(u[:], usf_d[:, :, cs].rearrange("g p t -> p g t"), R=[B.dr("usf", (g, c)) for g in range(4)], W=[u])
                for kc in range(4):
                    B.op("dve", lambda e, kc=kc: e.scalar_tensor_tensor(y[:, kc, :], u[:, kc, :], dsk[:, kc:kc + 1], y[:, kc, :], op0=ALU.mult, op1=ALU.add),
                         R=[u, dsk, y], W=[y])
                B.op("act", lambda e: e.activation(yf[:], y[:], AF.Gelu), R=[y], W=[yf])
                B.op("pool", lambda e: e.tensor_copy(yb[:], yf[:]), R=[yf], W=[yb])
                for fb in range(4):
                    pp = pg[n % 2]; s_ = sg[n % 2]
                    n += 1
                    for kc in range(4):
                        B.op("pe", lambda e, kc=kc, fb=fb, pp=pp: e.matmul(pp[:], lhsT=gw[:, kc, fb * 128:(fb + 1) * 128], rhs=yb[:, kc, :],
                                                                          start=(kc == 0), stop=(kc == 3)), R=[gw, yb], W=[pp])
                    B.op("act", lambda e, fb=fb, pp=pp, s_=s_: e.activation(s_[:], pp[:], AF.Sigmoid, bias=glub[:, fb:fb + 1], scale=1.0), R=[pp, glub], W=[s_])
                    B.op("dve", lambda e, fb=fb, s_=s_: e.tensor_tensor(s_o[:, fb, :], s_[:], yf[:, fb, :], op=ALU.mult), R=[s_, yf], W=[s_o])
                B.dma(s5T_d[:, :, cs].rearrange("g p t -> p g t"), s_o[:], R=[s_o], W=[B.dr("s5T", c)])
            B.barrier()

    def phase_merge(l, x_src, x_key):
        with ExitStack() as ph:
            mk = lambda shp, dt=F32, nm="t": B.sb(ph, shp, dt, nm)
            Wg = mk([128, 8, 3072], BF16)
            Wb = mk([128, 12, 1024], BF16)
            Wo = mk([128, 8, 1024], BF16)
            stage = [mk([128, 1024]) for _ in range(2)]
            load_w_bf16(ph, Wg, lambda kc: w_in[l, kc * 128:(kc + 1) * 128, 3072:6144], 3072, stage)
            load_w_bf16(ph, Wb, lambda a: w_branch[l, a // 4, (a % 4) * 128:(a % 4 + 1) * 128, :], 1024, stage, kcs=12)
            load_w_bf16(ph, Wo, lambda kc: w_o[l, kc * 128:(kc + 1) * 128, :], 1024, stage)
            bg = mk([128, 3, 8])
            B.dma(bg[:], b_gate[l], W=[bg])
            g1 = mk([128, D]); b1 = mk([128, D])
            B.dma(g1[:], ln1_g[l:l + 1, :].to_broadcast([128, D]), W=[g1])
            B.dma(b1[:], ln1_b[l:l + 1, :].to_broadcast([128, D]), W=[b1])
            lns = ln_scratch(ph)
            xc = [mk([128, 8, 512], BF16) for _ in range(2)]
            br = [[mk([128, 4, 512], BF16) for _ in range(3)] for _ in range(2)]
            mT = mk([128, 8, 512], BF16)
            gt = [mk([128, 512]) for _ in range(2)]
            acc = mk([128, 512]); tmp = [mk([128, 512]) for _ in range(2)]
            pgt = [B.ps(ph, [128, 512], F32, "pgt") for _ in range(2)]
            ppj = [B.ps(ph, [128, 512], F32, "ppj") for _ in range(2)]
            po = [B.ps(ph, [128, 512], F32, "po") for _ in range(2)]
            psT = B.ps(ph, [128, 1024], F32, "psT")
            xt = [mk([128, D]) for _ in range(2)]
            rr = [mk([128, D]) for _ in range(2)]
            x1c = [mk([128, 8, 512], BF16) for _ in range(2)]
            srcs = ((attT_d, "attT", True), (sguT_d, "sguT", False), (s5T_d, "s5T", False))
            n = 0
            for c in range(NCH):
                cs = slice(c * 512, (c + 1) * 512)
                xcc = xc[c % 2]
                B.dma(xcc[:], xT_d[:, :, cs].rearrange("k p t -> p k t"), R=[B.dr("xT", c)], W=[xcc])
                brc = br[c % 2]
                for i, (dd, nm, perh) in enumerate(srcs):
                    Rk = [B.dr(nm, (h, c)) for h in range(4)] if perh else [B.dr(nm, c)]
                    B.dma(brc[i][:], dd[:, :, cs].rearrange("g p t -> p g t"), R=Rk, W=[brc[i]])
                for db in range(8):
                    for i in range(3):
                        pg_ = pgt[n % 2]; pp = ppj[n % 2]; g_ = gt[n % 2]; t_ = tmp[n % 2]
                        n += 1
                        c0 = i * 1024 + db * 128
                        for kc in range(8):
                            B.op("pe", lambda e, kc=kc, c0=c0, pg_=pg_: e.matmul(pg_[:], lhsT=Wg[:, kc, c0:c0 + 128], rhs=xcc[:, kc, :],
                                                                              start=(kc == 0), stop=(kc == 7)), R=[Wg, xcc], W=[pg_])
                        B.op("act", lambda e, i=i, db=db, pg_=pg_, g_=g_: e.activation(g_[:], pg_[:], AF.Sigmoid, bias=bg[:, i, db:db + 1], scale=1.0),
                             R=[pg_, bg], W=[g_])
                        for kc in range(4):
                            B.op("pe", lambda e, kc=kc, i=i, db=db, pp=pp: e.matmul(pp[:], lhsT=Wb[:, i * 4 + kc, db * 128:(db + 1) * 128], rhs=brc[i][:, kc, :],
                                                                                 start=(kc == 0), stop=(kc == 3)), R=[Wb, brc[i]], W=[pp])
                        if i == 0:
                            B.op("dve", lambda e, pp=pp, g_=g_: e.tensor_tensor(acc[:], g_[:], pp[:], op=ALU.mult), R=[g_, pp], W=[acc])
                        else:
                            B.op("dve", lambda e, pp=pp, g_=g_, t_=t_: e.tensor_tensor(t_[:], g_[:], pp[:], op=ALU.mult), R=[g_, pp], W=[t_])
                            if i == 1:
                                B.op("pool", lambda e, t_=t_: e.tensor_tensor(acc[:], acc[:], t_[:], op=ALU.add), R=[acc, t_], W=[acc])
                            else:
                                B.op("pool", lambda e, t_=t_, db=db: e.tensor_tensor(mT[:, db, :], acc[:], t_[:], op=ALU.add), R=[acc, t_], W=[mT])
                x1cc = x1c[c % 2]
                for j in range(4):
                    t = c * 4 + j
                    x_ = xt[t % 2]; r_ = rr[t % 2]
                    B.dma(x_[:], x_src[t * 128:(t + 1) * 128, :], R=([B.dr(x_key, t)] if x_key else []), W=[x_])
                    for hf in range(2):
                        p_ = po[hf]
                        for kc in range(8):
                            B.op("pe", lambda e, kc=kc, hf=hf, p_=p_, j=j: e.matmul(p_[:], lhsT=mT[:, kc, j * 128:(j + 1) * 128], rhs=Wo[:, kc, hf * 512:(hf + 1) * 512],
                                                                                 start=(kc == 0), stop=(kc == 7)), R=[mT, Wo], W=[p_])
                        B.op("dve", lambda e, hf=hf, p_=p_, x_=x_, r_=r_: e.scalar_tensor_tensor(r_[:, hf * 512:(hf + 1) * 512], x_[:, hf * 512:(hf + 1) * 512], ALPHA, p_[:],
                                                                                               op0=ALU.mult, op1=ALU.add), R=[x_, p_], W=[r_])
                    layernorm(ph, r_, g1, b1, r_, D, lns)
                    B.dma(x1_d[t * 128:(t + 1) * 128, :], r_[:], R=[r_], W=[B.dr("x1", t)])
                    to_fm(ph, r_, psT, x1cc, j)
                B.dma(x1T_d[:, :, cs].rearrange("k p t -> p k t"), x1cc[:], R=[x1cc], W=[B.dr("x1T", c)])
            B.barrier()

    iot16 = B.sb(glob, [128, 16], F32, "iot16")
    B.op("pool", lambda e: e.iota(iot16[:], pattern=[[1, 16]], base=0, channel_multiplier=0, allow_small_or_imprecise_dtypes=True), W=[iot16])

    def phase_peer(l, last):
        puv = peer_uv.rearrange("l e d -> (l e) d")
        with ExitStack() as ph:
            mk = lambda shp, dt=F32, nm="t": B.sb(ph, shp, dt, nm)
            Wq = mk([128, 8, 2048], BF16)
            keysT = mk([128, 16, 128], BF16)
            pS = [B.ps(ph, [128, 512], F32, "pS") for _ in range(4)]
            pq = [B.ps(ph, [128, 512], F32, "pq") for _ in range(2)]
            psT = B.ps(ph, [128, 1024], F32, "psT")
            with ExitStack() as ph0:
                stage = [B.sb(ph0, [128, 1024], F32, "stg") for _ in range(2)]
                load_w_bf16(ph0, Wq, lambda kc: peer_w_q[l, kc * 128:(kc + 1) * 128, :], 2048, stage)
                kf = B.sb(ph0, [128, 16, 128], F32, "kf")
                B.dma(kf[:], peer_keys[l].rearrange("a k d -> k a d"), W=[kf])
                for a in range(16):
                    B.op("pe", lambda e, a=a: e.transpose(pS[a // 4][:, (a % 4) * 128:(a % 4 + 1) * 128], kf[:, a, :], ident[:]), R=[kf, ident], W=[pS[a // 4]])
                for b_ in range(4):
                    B.op("act", lambda e, b_=b_: e.copy(keysT[:, b_ * 4:(b_ + 1) * 4, :].rearrange("p a k -> p (a k)"), pS[b_][:]), R=[pS[b_]], W=[keysT])
                B.barrier()
            g2 = mk([128, D]); b2 = mk([128, D])
            B.dma(g2[:], ln2_g[l:l + 1, :].to_broadcast([128, D]), W=[g2])
            B.dma(b2[:], ln2_b[l:l + 1, :].to_broadcast([128, D]), W=[b2])
            lns = ln_scratch(ph)
            x1c = [mk([128, 8, 512], BF16) for _ in range(2)]
            qT = mk([128, 16, 512], BF16)
            xt = [mk([128, D]) for _ in range(2)]
            S = mk([128, 16, 128]); S2 = mk([128, 16, 128])
            v8 = mk([128, 16, 16]); i8 = mk([128, 16, 16], U32); sif = mk([128, 16, 16])
            cand = mk([128, 8, 16, 16]); cand2 = mk([128, 8, 256])
            sc = mk([128, 8, 16]); fl = mk([128, 8, 16], U32); fi_ = mk([128, 8, 16], U32); fj_ = mk([128, 8, 16], U32)
            fif = mk([128, 8, 16]); fjf = mk([128, 8, 16])
            oh = mk([128, 8, 16, 16]); e1 = mk([128, 8, 16]); e2 = mk([128, 8, 16])
            eid = mk([128, 128], U32)
            ex = mk([128, 8, 16]); zz = mk([128, 8]); gg = mk([128, 128])
            NUG = 10
            Ug = [mk([128, 2 * D], BF16, "Ug") for _ in range(NUG)]
            Dg = [mk([128, 128], BF16, "Dg") for _ in range(3)]
            hs = [mk([128, 1], F32, "hs") for _ in range(4)]
            ws = [mk([128, 1], F32, "ws") for _ in range(4)]
            rr = [mk([128, D]) for _ in range(2)]
            EMAX = nc.gpsimd.to_reg(DEPTH * nexp - 1)
            x2c = [mk([128, 8, 512], BF16) for _ in range(1)]
            ng = 0
            for c in range(NCH):
                cs = slice(c * 512, (c + 1) * 512)
                xcc = x1c[c % 2]
                B.dma(xcc[:], x1T_d[:, :, cs].rearrange("k p t -> p k t"), R=[B.dr("x1T", c)], W=[xcc])
                for a in range(16):
                    pp = pq[a % 2]
                    for kc in range(8):
                        B.op("pe", lambda e, kc=kc, a=a, pp=pp: e.matmul(pp[:], lhsT=Wq[:, kc, a * 128:(a + 1) * 128], rhs=xcc[:, kc, :],
                                                                        start=(kc == 0), stop=(kc == 7)), R=[Wq, xcc], W=[pp])
                    B.op("act", lambda e, a=a, pp=pp: e.copy(qT[:, a, :], pp[:]), R=[pp], W=[qT])
                x2cc = x2c[0]
                for j in range(4):
                    t = c * 4 + j
                    js = slice(j * 128, (j + 1) * 128)
                    x_ = xt[t % 2]
                    B.dma(x_[:], x1_d[t * 128:(t + 1) * 128, :], R=[B.dr("x1", t)], W=[x_])
                    for a in range(16):
                        B.op("pe", lambda e, a=a: e.matmul(pS[a // 4][:, (a % 4) * 128:(a % 4 + 1) * 128], lhsT=qT[:, a, js], rhs=keysT[:, a, :],
                                                          start=True, stop=True), R=[qT, keysT], W=[pS[a // 4]])
                    for b_ in range(4):
                        B.op("act", lambda e, b_=b_: e.copy(S[:, b_ * 4:(b_ + 1) * 4, :].rearrange("p a k -> p (a k)"), pS[b_][:]), R=[pS[b_]], W=[S])
                    for a in range(16):
                        B.op("dve", lambda e, a=a: e.max(out=v8[:, a, 0:8], in_=S[:, a, :]), R=[S], W=[v8])
                        B.op("dve", lambda e, a=a: e.max_index(i8[:, a, 0:8], v8[:, a, 0:8], S[:, a, :]), R=[S, v8], W=[i8])
                        B.op("dve", lambda e, a=a: e.match_replace(out=S2[:, a, :], in_to_replace=v8[:, a, 0:8], in_values=S[:, a, :], imm_value=NEG),
                             R=[S, v8], W=[S2])
                        B.op("dve", lambda e, a=a: e.max(out=v8[:, a, 8:16], in_=S2[:, a, :]), R=[S2], W=[v8])
                        B.op("dve", lambda e, a=a: e.max_index(i8[:, a, 8:16], v8[:, a, 8:16], S2[:, a, :]), R=[S2, v8], W=[i8])
                    B.op("dve", lambda e: e.tensor_copy(sif[:], i8[:]), R=[i8], W=[sif])
                    v8v = v8[:].rearrange("p (h c) k -> p h c k", c=2)
                    siv = sif[:].rearrange("p (h c) k -> p h c k", c=2)
                    B.op("dve", lambda e: e.tensor_tensor(cand[:], v8v[:, :, 0, :].unsqueeze(3).to_broadcast([128, 8, 16, 16]),
                                                          v8v[:, :, 1, :].unsqueeze(2).to_broadcast([128, 8, 16, 16]), op=ALU.add), R=[v8], W=[cand])
                    candf = cand[:].rearrange("p h i j -> p h (i j)")
                    for h in range(8):
                        B.op("dve", lambda e, h=h: e.max(out=sc[:, h, 0:8], in_=candf[:, h, :]), R=[cand], W=[sc])
                        B.op("dve", lambda e, h=h: e.max_index(fl[:, h, 0:8], sc[:, h, 0:8], candf[:, h, :]), R=[cand, sc], W=[fl])
                        B.op("dve", lambda e, h=h: e.match_replace(out=cand2[:, h, :], in_to_replace=sc[:, h, 0:8], in_values=candf[:, h, :], imm_value=NEG),
                             R=[cand, sc], W=[cand2])
                        B.op("dve", lambda e, h=h: e.max(out=sc[:, h, 8:16], in_=cand2[:, h, :]), R=[cand2], W=[sc])
                        B.op("dve", lambda e, h=h: e.max_index(fl[:, h, 8:16], sc[:, h, 8:16], cand2[:, h, :]), R=[cand2, sc], W=[fl])
                    B.op("dve", lambda e: e.tensor_single_scalar(fi_[:], fl[:], 4, op=ALU.logical_shift_right), R=[fl], W=[fi_])
                    B.op("dve", lambda e: e.tensor_single_scalar(fj_[:], fl[:], 15, op=ALU.bitwise_and), R=[fl], W=[fj_])
                    B.op("dve", lambda e: e.tensor_copy(fif[:], fi_[:]), R=[fi_], W=[fif])
                    B.op("dve", lambda e: e.tensor_copy(fjf[:], fj_[:]), R=[fj_], W=[fjf])
                    io_b = iot16[:].unsqueeze(1).unsqueeze(1).to_broadcast([128, 8, 16, 16])
                    for (ff, cc_, eo) in ((fif, 0, e1), (fjf, 1, e2)):
                        B.op("dve", lambda e, ff=ff: e.tensor_tensor(oh[:], io_b, ff[:].unsqueeze(3).to_broadcast([128, 8, 16, 16]), op=ALU.is_equal),
                             R=[iot16, ff], W=[oh])
                        B.op("dve", lambda e, cc_=cc_: e.tensor_tensor(oh[:], oh[:], siv[:, :, cc_, :].unsqueeze(2).to_broadcast([128, 8, 16, 16]), op=ALU.mult),
                             R=[oh, sif], W=[oh])
                        B.op("dve", lambda e, eo=eo: e.tensor_reduce(eo[:], oh[:], axis=AX.X, op=ALU.add), R=[oh], W=[eo])
                    B.op("dve", lambda e: e.scalar_tensor_tensor(e1[:].rearrange("p h k -> p (h k)"), e1[:].rearrange("p h k -> p (h k)"), 128.0,
                                                                 e2[:].rearrange("p h k -> p (h k)"), op0=ALU.mult, op1=ALU.add), R=[e1, e2], W=[e1])
                    B.op("dve", lambda e: e.tensor_scalar(e1[:], e1[:], float(l * nexp), eoff[:, 0:1], op0=ALU.add, op1=ALU.add), R=[e1, eoff], W=[e1])
                    B.op("dve", lambda e: e.tensor_copy(eid[:], e1[:].rearrange("p h k -> p (h k)")), R=[e1], W=[eid])
                    B.op("dve", lambda e: e.tensor_tensor(ex[:], sc[:], sc[:, :, 0:1].to_broadcast([128, 8, 16]), op=ALU.subtract), R=[sc], W=[ex])
                    B.op("act", lambda e: e.activation(ex[:], ex[:], AF.Exp), R=[ex], W=[ex])
                    B.op("dve", lambda e: e.tensor_reduce(zz[:], ex[:], axis=AX.X, op=ALU.add), R=[ex], W=[zz])
                    B.op("dve", lambda e: e.reciprocal(zz[:], zz[:]), R=[zz], W=[zz])
                    B.op("dve", lambda e: e.tensor_tensor(gg[:].rearrange("p (h k) -> p h k", h=8), ex[:], zz[:].unsqueeze(2).to_broadcast([128, 8, 16]), op=ALU.mult),
                         R=[ex, zz], W=[gg])
                    junk = S2[:, 0:8, :].rearrange("p a k -> p (a k)")
                    pend = None
                    for hk in range(128):
                        ug = Ug[ng % NUG]
                        ng += 1
                        h_ = hs[hk % 4]; w_ = ws[hk % 4]
                        B.dma(ug[:], puv, R=[eid], W=[ug], Q="pool", ind=bass.IndirectOffsetOnAxis(ap=eid[:, hk:hk + 1], axis=0), bc=EMAX)
                        B.op("dve", lambda e, ug=ug, h_=h_: e.scalar_tensor_tensor(junk, ug[:, 0:D], 1.0, x_[:], op0=ALU.mult, op1=ALU.mult,
                                                                                 accum_out=h_[:, 0:1]), R=[ug, x_], W=[S2, h_])
                        B.op("act", lambda e, h_=h_, w_=w_: e.activation(w_[:], h_[:], AF.Gelu), R=[h_], W=[w_])
                        dg = Dg[hk % 3]

                        def tail(hk=hk, dg=dg, vb=ug, w_=w_):
                            B.op("dve", lambda e: e.tensor_scalar(dg[:], ident[:], w_[:, 0:1], gg[:, hk:hk + 1], op0=ALU.mult, op1=ALU.mult),
                                 R=[ident, w_, gg], W=[dg])
                            for hf in range(2):
                                B.op("pe", lambda e, hf=hf: e.matmul(pq[hf][:], lhsT=dg[:], rhs=vb[:, D + hf * 512:D + (hf + 1) * 512],
                                                                     start=(hk == 0), stop=(hk == 127)), R=[dg, vb], W=[pq[hf]])
                        if pend is not None:
                            pend()
                        pend = tail
                    pend()
                    pend = None
                    r_ = rr[t % 2]
                    for hf in range(2):
                        B.op("dve", lambda e, r_=r_, hf=hf: e.scalar_tensor_tensor(r_[:, hf * 512:(hf + 1) * 512], x_[:, hf * 512:(hf + 1) * 512], ALPHA, pq[hf][:],
                                                                                 op0=ALU.mult, op1=ALU.add), R=[x_, pq[hf]], W=[r_])
                    layernorm(ph, r_, g2, b2, r_, D, lns)
                    if last:
                        B.dma(y_out[t * 128:(t + 1) * 128, :], r_[:], R=[r_], W=[B.dr("y", t)])
                    else:
                        B.dma(x2_d[t * 128:(t + 1) * 128, :], r_[:], R=[r_], W=[B.dr("x2", t)])
                        to_fm(ph, r_, psT, x2cc, j)
                if not last:
                    B.dma(xT_d[:, :, cs].rearrange("k p t -> p k t"), x2cc[:], R=[x2cc], W=[B.dr("xT", c)])
            B.barrier()

    phase_x0()
    for l in range(depth):
        if stop_after == "x0":
            break
        phase_inproj(l)
        if stop_after == "inproj":
            break
        if "attn" not in SKIP:
            phase_attn(l)
        if stop_after == "attn":
            break
        phase_s5(l)
        if stop_after == "s5":
            break
        phase_merge(l, x_in if l == 0 else x2_d, None if l == 0 else "x2")
        if stop_after == "merge":
            break
        phase_peer(l, l == depth - 1)
    B.barrier()
    B.es.close()
    return nc


def host_consts(Lc, nseg):
    seglen = Lc // nseg
    t = np.arange(Lc)
    seg = (t // seglen).astype(np.float64)
    tl = (t % seglen)
    a = (tl >> 6).astype(np.float64)
    b = (tl & 63).astype(np.float64)
    qaug = np.zeros((NH, 6, Lc), np.float32)
    kaug = np.zeros((NH, 2, 6, Lc), np.float32)
    negb = np.zeros((128, NH, 128), np.float32)
    i = np.arange(128)
    for h in range(NH):
        s = SLOPES[h]
        qaug[h, 0] = -s * 64 * a
        qaug[h, 1] = -s * b
        qaug[h, 2] = -s * SEG_OFF * seg
        qaug[h, 3:6] = 1.0
        kaug[h, 0, 0:3] = 1.0
        kaug[h, 0, 3] = s * 64 * a
        kaug[h, 0, 4] = s * b
        kaug[h, 0, 5] = s * SEG_OFF * seg
        kaug[h, 1] = -kaug[h, 0]
        negb[:, h, :] = -s * np.abs(i[:, None] - i[None, :])
    bmask = np.zeros((128, 2), np.float32)
    p = np.arange(128)
    bmask[:, 0] = ((p // 16) % 2 == 0)
    bmask[:, 1] = ((p // 16) % 2 == 1)
    return {"qaug": qaug.astype(ml_dtypes.bfloat16), "kaug": kaug.astype(ml_dtypes.bfloat16), "negb": negb, "bmask": bmask}


def host_weights(inp):
    f = lambda k: np.ascontiguousarray(np.asarray(inp[k], dtype=np.float32))
    out = {}
    out["b_gate"] = np.ascontiguousarray(f("b_gate").reshape(DEPTH, 3, 8, 128).transpose(0, 3, 1, 2))
    out["s5_d"] = np.ascontiguousarray(f("s5_d").reshape(DEPTH, 4, 128).transpose(0, 2, 1))
    out["s5_glu_b"] = np.ascontiguousarray(f("s5_glu_b").reshape(DEPTH, 4, 128).transpose(0, 2, 1))
    for k in ("w_in", "att_norm_g", "sgu_ln_g", "sgu_ln_b", "sgu_w_s", "s5_glu_w", "w_branch",
              "w_o", "ln1_g", "ln1_b", "peer_w_q", "ln2_g", "ln2_b"):
        out[k] = f(k)
    out["peer_uv"] = np.concatenate([f("peer_u"), f("peer_v")], axis=-1)
    out["lam4"] = np.ascontiguousarray(np.stack([f("lambda_q1"), f("lambda_k1"), f("lambda_q2"), f("lambda_k2")], axis=1))
    out["sgu_b_s"] = f("sgu_b_s").reshape(DEPTH, 512)
    out["peer_keys"] = f("peer_keys").reshape(DEPTH, 16, 128, 128)
    are, aim, ls = f("s5_a_re"), f("s5_a_im"), f("s5_log_step")
    lsb = np.broadcast_to(ls[..., None], are.shape)
    def st_layout(a):
        a = a.reshape(DEPTH, 2, 16, 2, 64)
        return a.transpose(0, 3, 4, 1, 2).reshape(DEPTH, 128, 32)
    out["s5p"] = np.ascontiguousarray(np.stack([st_layout(are), st_layout(aim), st_layout(lsb)], axis=2))
    def ch_layout(a):
        a = a.reshape(DEPTH, 2, 4, 8, 64, 16)
        return a.transpose(0, 3, 5, 1, 2, 4).reshape(DEPTH, 128, 512)
    bc16 = lambda a: np.broadcast_to(a[..., None], a.shape + (16,))
    out["s5q"] = np.ascontiguousarray(np.stack([ch_layout(bc16(are)), ch_layout(bc16(aim)), ch_layout(bc16(lsb))], axis=2))
    out["s5b"] = np.ascontiguousarray(np.stack([ch_layout(f("s5_b_re")), ch_layout(f("s5_b_im"))], axis=2))
    cc = np.zeros((DEPTH, 2, 64, 2, 2, 16, 2, 16), np.float32)
    for ri, key in enumerate(("s5_c_re", "s5_c_im")):
        c = f(key).reshape(DEPTH, 2, 16, 2, 16, 64)
        for gl in range(2):
            cc[:, gl, :, ri, :, :, gl, :] = c[:, :, :, gl, :, :].transpose(0, 4, 1, 2, 3)
    out["s5c"] = np.ascontiguousarray(cc.reshape(DEPTH, 128, 2, 32, 32))
    return out


_CACHE = {}


def kernel(**inputs):
    xp = np.asarray(inputs["x_prompt"], dtype=np.float32)
    xs = np.asarray(inputs["x_sample"], dtype=np.float32)
    Lc = xp.shape[1]
    wts = host_weights(inputs)
    cp = host_consts(Lc, 1)
    cs = host_consts(Lc, 2)
    one = np.ones((128, 1), np.float32)
    zero = np.zeros((128, 1), np.float32)
    streams = [xp[0], xp[1], xs[0:2].reshape(Lc, D), xs[2:4].reshape(Lc, D)]
    in_maps = []
    for c in range(8):
        si = c if c < 4 else 2 + (c % 2)
        m = dict(wts)
        m["x"] = np.ascontiguousarray(streams[si])
        m.update(cp if si < 2 else cs)
        m["gate"] = one if si < 2 else zero
        m["eoff"] = zero if c < 4 else np.full((128, 1), 1.0e6, np.float32)
        in_maps.append(m)
    if "nc" not in _CACHE:
        _CACHE["nc"] = build(Lc)
    res = run_bass_kernel_spmd(_CACHE["nc"], in_maps, core_ids=list(range(8)))
    r = res.results
    y_prompt = np.stack([r[0]["y"], r[1]["y"]], axis=0).astype(np.float32)
    y_sample = np.concatenate([r[2]["y"].reshape(2, Lc // 2, D), r[3]["y"].reshape(2, Lc // 2, D)], axis=0).astype(np.float32)
    return (y_prompt, y_sample)
```

```python
import math
from contextlib import ExitStack
import numpy as np
import ml_dtypes
import concourse.bass as bass
import concourse.mybir as mybir
from concourse.bass_utils import run_bass_kernel_spmd

F32 = mybir.dt.float32
BF16 = mybir.dt.bfloat16
U32 = mybir.dt.uint32
I32 = mybir.dt.int32
AF = mybir.ActivationFunctionType
ALU = mybir.AluOpType
AX = mybir.AxisListType

D = 1024
DEPTH = 4
NH = 4
LN_EPS = 1e-5
ALPHA = (2 * DEPTH) ** 0.25
SLOPES = [2.0 ** (-8.0 * (h + 1) / NH) for h in range(NH)]
SEG_OFF = 16384.0
S5T = 512
NEG = -1.0e30


class Buf:
    __slots__ = ("w", "rs")

    def __init__(self):
        self.w = None
        self.rs = {}


class T:
    def __init__(self, t, excl=False):
        self.t = t
        self.b = Buf()
        self.excl = excl

    def __getitem__(self, k):
        return self.t[k]


def _b(x):
    return x.b if isinstance(x, T) else x


class Bld:
    def __init__(self):
        nc = bass.Bass("TRN2", target_bir_lowering=False)
        self.nc = nc
        self.es = ExitStack()
        self.eng = {"pe": nc.tensor, "act": nc.scalar, "dve": nc.vector, "pool": nc.gpsimd, "sp": nc.sync}
        self.sem = {}
        self.cnt = {}
        for e in ("pe", "act", "dve", "pool"):
            self.sem[e] = self.es.enter_context(nc.semaphore("s_" + e))
            self.cnt[e] = 0
        self.seen = {e: {} for e in self.eng}
        self.dkeys = {"sp": [], "pool": []}
        for q, n in (("sp", 48), ("pool", 24)):
            for i in range(n):
                k = ("d", q, i)
                self.sem[k] = self.es.enter_context(nc.semaphore("sd_%s%d" % (q, i)))
                self.cnt[k] = 0
                self.dkeys[q].append(k)
        self.drr = {"sp": 0, "pool": 0}
        self.dreg = {}
        self.nid = 0

    def sb(self, ctx, shape, dt, name=None):
        self.nid += 1
        return T(ctx.enter_context(self.nc.sbuf_tensor("%s_%d" % (name or "sb", self.nid), list(shape), dt)))

    def ps(self, ctx, shape, dt, name=None):
        self.nid += 1
        return T(ctx.enter_context(self.nc.psum_tensor("%s_%d" % (name or "ps", self.nid), list(shape), dt)), excl=True)

    def dr(self, name, key):
        k = (name, key)
        b = self.dreg.get(k)
        if b is None:
            b = self.dreg[k] = Buf()
        return b

    def _deps(self, R, W):
        deps = []
        for r in R:
            b = _b(r)
            if b.w:
                deps.append(b.w)
            if isinstance(r, T) and r.excl:
                deps.extend(b.rs.items())
        for w in W:
            b = _b(w)
            if b.w:
                deps.append(b.w)
            deps.extend(b.rs.items())
        return deps

    def _wait(self, E, deps):
        seen = self.seen[E]
        for k, v in deps:
            if k == E and E == "pe":
                continue
            if seen.get(k, 0) >= v:
                continue
            self.eng[E].wait_ge(self.sem[k], v)
            seen[k] = v

    def _mark(self, tok, R, W):
        k, v = tok
        for r in R:
            b = _b(r)
            if b.rs.get(k, 0) < v:
                b.rs[k] = v
        for w in W:
            b = _b(w)
            b.w = tok
            b.rs = {}

    def op(self, E, fn, R=(), W=()):
        self._wait(E, self._deps(R, W))
        ins = fn(self.eng[E])
        self.cnt[E] += 1
        ins.then_inc(self.sem[E], 1)
        tok = (E, self.cnt[E])
        self._mark(tok, R, W)
        return tok

    def dma(self, out, in_, R=(), W=(), Q="sp", ind=None, bc=None):
        self._wait(Q, self._deps(R, W))
        i = self.drr[Q]
        self.drr[Q] = (i + 1) % len(self.dkeys[Q])
        key = self.dkeys[Q][i]
        if ind is None:
            ins = self.eng[Q].dma_start(out=out, in_=in_)
        else:
            ins = self.eng[Q].indirect_dma_start(out=out, out_offset=None, in_=in_, in_offset=ind, bounds_check=bc, oob_is_err=False)
        self.cnt[key] += 16
        ins.then_inc(self.sem[key], 16)
        tok = (key, self.cnt[key])
        self._mark(tok, R, W)
        return tok

    def barrier(self):
        allk = [(k, v) for k, v in self.cnt.items() if v > 0]
        for E in self.eng:
            self._wait(E, allk)


SKIP = set()
DBG = {"fm": True, "v": True, "zv": True, "ln": True, "sgu": True}


def build(Lc, depth=DEPTH, debug=False, stop_after=None, nexp=16384):
    B = Bld()
    nc = B.nc
    NT = Lc // 128
    NCH = Lc // 512
    HALF_T = NT // 2
    kind_s = "ExternalOutput" if debug else "Internal"

    def din(name, shape, dt=F32):
        return nc.dram_tensor(name, list(shape), dt, kind="ExternalInput").ap()

    def dsc(name, shape, dt=F32):
        return nc.dram_tensor(name, list(shape), dt, kind=kind_s).ap()

    x_in = din("x", [Lc, D])
    gate_in = din("gate", [128, 1])
    eoff_in = din("eoff", [128, 1])
    qaug_in = din("qaug", [NH, 6, Lc], BF16)
    kaug_in = din("kaug", [NH, 2, 6, Lc], BF16)
    negb_in = din("negb", [128, NH, 128])
    bmask_in = din("bmask", [128, 2])
    w_in = din("w_in", [DEPTH, D, 6144])
    b_gate = din("b_gate", [DEPTH, 128, 3, 8])
    lam_in = din("lam4", [DEPTH, 4, 64])
    att_norm_g = din("att_norm_g", [DEPTH, 128])
    sgu_ln_g = din("sgu_ln_g", [DEPTH, 512])
    sgu_ln_b = din("sgu_ln_b", [DEPTH, 512])
    sgu_w_s = din("sgu_w_s", [DEPTH, 4, 128, 128])
    sgu_b_s = din("sgu_b_s", [DEPTH, 512])
    s5p_in = din("s5p", [DEPTH, 128, 3, 32])
    s5q_in = din("s5q", [DEPTH, 128, 3, 512])
    s5b_in = din("s5b", [DEPTH, 128, 2, 512])
    s5c_in = din("s5c", [DEPTH, 128, 2, 32, 32])
    s5_d = din("s5_d", [DEPTH, 128, 4])
    s5_glu_w = din("s5_glu_w", [DEPTH, 512, 512])
    s5_glu_b = din("s5_glu_b", [DEPTH, 128, 4])
    w_branch = din("w_branch", [DEPTH, 3, 512, D])
    w_o = din("w_o", [DEPTH, D, D])
    ln1_g = din("ln1_g", [DEPTH, D])
    ln1_b = din("ln1_b", [DEPTH, D])
    peer_w_q = din("peer_w_q", [DEPTH, D, 2048])
    peer_keys = din("peer_keys", [DEPTH, 16, 128, 128])
    peer_uv = din("peer_uv", [DEPTH, nexp, 2 * D])
    ln2_g = din("ln2_g", [DEPTH, D])
    ln2_b = din("ln2_b", [DEPTH, D])
    y_out = nc.dram_tensor("y", [Lc, D], F32, kind="ExternalOutput").ap()

    xT_d = dsc("xT", [8, 128, Lc], BF16)
    x1_d = dsc("x1", [Lc, D])
    x1T_d = dsc("x1T", [8, 128, Lc], BF16)
    x2_d = dsc("x2", [Lc, D])
    qT_d = dsc("qT", [NH, 2, 64, Lc], BF16)
    kT_d = dsc("kT", [NH, 2, 64, Lc], BF16)
    va_d = dsc("vaug", [NT, 128, NH, 129], BF16)
    usb_d = dsc("usb", [4, 128, Lc], BF16)
    usf_d = dsc("usf", [4, 128, Lc])
    attT_d = dsc("attT", [4, 128, Lc], BF16)
    sguT_d = dsc("sguT", [4, 128, Lc], BF16)
    yT_d = dsc("yT", [16, 32, Lc])
    s5T_d = dsc("s5T", [4, 128, Lc], BF16)

    glob = B.es
    ident = B.sb(glob, [128, 128], F32, "ident")
    identb = B.sb(glob, [128, 128], BF16, "identb")
    iot = B.sb(glob, [128, 128], F32, "iot")
    B.op("pool", lambda e: e.iota(iot[:], pattern=[[1, 128]], base=0, channel_multiplier=-1,
                                  allow_small_or_imprecise_dtypes=True), W=[iot])
    B.op("dve", lambda e: e.tensor_scalar(ident[:], iot[:], 0.0, None, op0=ALU.is_equal), R=[iot], W=[ident])
    B.op("dve", lambda e: e.tensor_copy(identb[:], ident[:]), R=[ident], W=[identb])
    gate = B.sb(glob, [128, 1], F32, "gate")
    B.dma(gate[:], gate_in[:, :], W=[gate])
    eoff = B.sb(glob, [128, 1], F32, "eoff")
    B.dma(eoff[:], eoff_in[:, :], W=[eoff])
    eps_t = B.sb(glob, [128, 1], F32, "eps")
    B.op("dve", lambda e: e.memset(eps_t[:], LN_EPS), W=[eps_t])

    def layernorm(ph, xt, g_bc, b_bc, out, width, scr):
        nchk = width // 512
        st = scr["st"]
        mv = scr["mv"]
        rs = scr["rs"]
        for i in range(nchk):
            B.op("dve", lambda e, i=i: e.bn_stats(st[:, i, :], xt[:, i * 512:(i + 1) * 512]), R=[xt], W=[st])
        B.op("dve", lambda e: e.bn_aggr(mv[:], st[:, 0:nchk, :]), R=[st], W=[mv])
        B.op("act", lambda e: e.activation(rs[:], mv[:, 1:2], AF.Sqrt, bias=eps_t[:, 0:1], scale=1.0), R=[mv, eps_t], W=[rs])
        B.op("dve", lambda e: e.reciprocal(rs[:], rs[:]), R=[rs], W=[rs])
        B.op("dve", lambda e: e.tensor_scalar(xt[:, 0:width], xt[:, 0:width], mv[:, 0:1], rs[:, 0:1],
                                              op0=ALU.subtract, op1=ALU.mult), R=[xt, mv, rs], W=[xt])
        B.op("dve", lambda e: e.tensor_tensor(xt[:, 0:width], xt[:, 0:width], g_bc[:, 0:width], op=ALU.mult), R=[xt, g_bc], W=[xt])
        B.op("dve", lambda e: e.tensor_tensor(out[:, 0:width], xt[:, 0:width], b_bc[:, 0:width], op=ALU.add), R=[xt, b_bc], W=[out])

    def ln_scratch(ph):
        return {"st": B.sb(ph, [128, 2, 6], F32, "lnst"), "mv": B.sb(ph, [128, 2], F32, "lnmv"),
                "rs": B.sb(ph, [128, 1], F32, "lnrs")}

    def load_w_bf16(ph, dst, src_rows, ncols, stage, col0=0, kcs=8):
        step = stage[0].t.shape[1]
        n = 0
        for kc in range(kcs):
            for c0 in range(0, ncols, step):
                cw = min(step, ncols - c0)
                stg = stage[n % len(stage)]
                B.dma(stg[:, 0:cw], src_rows(kc)[:, c0:c0 + cw], W=[stg])
                eng = "pool" if n % 2 == 0 else "dve"
                B.op(eng, lambda e, kc=kc, c0=c0, cw=cw, stg=stg: e.tensor_copy(dst[:, kc, col0 + c0:col0 + c0 + cw], stg[:, 0:cw]),
                     R=[stg], W=[dst])
                n += 1

    def to_fm(ph, src, psT, dstc, j, nk=8):
        for k in range(nk):
            B.op("pe", lambda e, k=k: e.transpose(psT[:, k * 128:(k + 1) * 128], src[:, k * 128:(k + 1) * 128], ident[:]),
                 R=[src, ident], W=[psT])
        B.op("act", lambda e: e.copy(dstc[:, 0:nk, j * 128:(j + 1) * 128], psT[:, 0:nk * 128].rearrange("p (k t) -> p k t", k=nk)),
             R=[psT], W=[dstc])

    def phase_x0():
        with ExitStack() as ph:
            xin = [B.sb(ph, [128, D], F32, "xin") for _ in range(2)]
            xc = [B.sb(ph, [128, 8, 512], BF16, "xc") for _ in range(2)]
            psT = [B.ps(ph, [128, 1024], F32, "psT") for _ in range(2)]
            for c in range(NCH):
                cb = xc[c % 2]
                for j in range(4):
                    t = c * 4 + j
                    xi = xin[t % 2]
                    B.dma(xi[:], x_in[t * 128:(t + 1) * 128, :], W=[xi])
                    to_fm(ph, xi, psT[t % 2], cb, j)
                B.dma(xT_d[:, :, c * 512:(c + 1) * 512].rearrange("k p t -> p k t"), cb[:], R=[cb],
                      W=[B.dr("xT", c)])
            B.barrier()

    def phase_inproj(l):
        with ExitStack() as ph:
            W = B.sb(ph, [128, 8, 3072], BF16, "W")
            stage = [B.sb(ph, [128, 1536], F32, "stg") for _ in range(2)]
            load_w_bf16(ph, W, lambda kc: w_in[l, kc * 128:(kc + 1) * 128, 0:3072], 3072, stage)
            lng = B.sb(ph, [128, 512], F32, "lng")
            lnb = B.sb(ph, [128, 512], F32, "lnb")
            bsb = B.sb(ph, [128, 512], F32, "bsb")
            B.dma(lng[:], sgu_ln_g[l:l + 1, :].to_broadcast([128, 512]), W=[lng])
            B.dma(lnb[:], sgu_ln_b[l:l + 1, :].to_broadcast([128, 512]), W=[lnb])
            B.dma(bsb[:], sgu_b_s[l:l + 1, :].to_broadcast([128, 512]), W=[bsb])
            wsf = B.sb(ph, [128, 4, 128], F32, "wsf")
            wsT = B.sb(ph, [128, 4, 128], BF16, "wsT")
            B.dma(wsf[:], sgu_w_s[l].rearrange("g t s -> t g s"), W=[wsf])
            psT = B.ps(ph, [128, 512], F32, "psT")
            for g in range(4):
                B.op("pe", lambda e, g=g: e.transpose(psT[:, g * 128:(g + 1) * 128], wsf[:, g, :], ident[:]), R=[wsf, ident], W=[psT])
            B.op("act", lambda e: e.copy(wsT[:].rearrange("p g t -> p (g t)"), psT[:]), R=[psT], W=[wsT])
            lns = ln_scratch(ph)
            pfm = [B.ps(ph, [128, 512], F32, "pfm") for _ in range(3)]
            ptm = [B.ps(ph, [128, 512], F32, "ptm") for _ in range(2)]
            psg = B.ps(ph, [128, 512], F32, "psg")
            xc = [B.sb(ph, [128, 8, 512], BF16, "xc") for _ in range(2)]
            qk = [B.sb(ph, [128, 512], BF16, "qk") for _ in range(3)]
            uT = [B.sb(ph, [128, 4, 512], F32, "uT") for _ in range(2)]
            usf = [B.sb(ph, [128, 512], F32, "usf") for _ in range(2)]
            usb = [B.sb(ph, [128, 512], BF16, "usb") for _ in range(2)]
            vau = [B.sb(ph, [128, NH, 129], BF16, "vau") for _ in range(2)]
            for v in vau:
                B.op("pool", lambda e, v=v: e.memset(v[:], 1.0), W=[v])
            zv = [B.sb(ph, [128, 512], F32, "zv") for _ in range(2)]
            vln = [B.sb(ph, [128, 512], BF16, "vln") for _ in range(2)]
            svt = [B.sb(ph, [128, 512], F32, "svt") for _ in range(2)]
            sgc = [B.sb(ph, [128, 4, 512], BF16, "sgc") for _ in range(2)]
            nfm = 0
            ntm = 0
            for c in range(NCH):
                cs = slice(c * 512, (c + 1) * 512)
                xcc = xc[c % 2]
                B.dma(xcc[:], xT_d[:, :, cs].rearrange("k p t -> p k t"), R=[B.dr("xT", c)], W=[xcc])
                uTc = uT[c % 2]
                for blk in range(24 if DBG["fm"] else 0):
                    if 8 <= blk < 12:
                        continue
                    if 16 <= blk < 20:
                        continue
                    col0 = blk * 128
                    pp = pfm[nfm % 3]
                    for kc in range(8):
                        B.op("pe", lambda e, kc=kc, pp=pp, col0=col0: e.matmul(pp[:], lhsT=W[:, kc, col0:col0 + 128], rhs=xcc[:, kc, :],
                                                                               start=(kc == 0), stop=(kc == 7)), R=[W, xcc], W=[pp])
                    if blk < 8:
                        o = qk[nfm % 3]
                        sc = 0.125 if blk < 4 else 1.0
                        B.op("act", lambda e, o=o, pp=pp, sc=sc: e.activation(o[:], pp[:], AF.Copy, scale=sc), R=[pp], W=[o])
                        dst = qT_d if blk < 4 else kT_d
                        h = blk % 4
                        B.dma(dst[h, :, :, cs].rearrange("c d t -> (c d) t"), o[:], R=[o], W=[B.dr("qk%d" % (blk // 4), (h, c))])
                    elif blk < 16:
                        g = blk - 12
                        B.op("act", lambda e, g=g, pp=pp: e.activation(uTc[:, g, :], pp[:], AF.Gelu), R=[pp], W=[uTc])
                    else:
                        g = blk - 20
                        of = usf[g % 2]
                        ob = usb[g % 2]
                        B.op("act", lambda e, of=of, pp=pp: e.copy(of[:], pp[:]), R=[pp], W=[of])
                        B.op("dve", lambda e, ob=ob, of=of: e.tensor_copy(ob[:], of[:]), R=[of], W=[ob])
                        B.dma(usf_d[g, :, cs], of[:], R=[of], W=[B.dr("usf", (g, c))])
                        B.dma(usb_d[g, :, cs], ob[:], R=[ob], W=[B.dr("usb", (g, c))])
                    nfm += 1
                sg = sgc[c % 2]
                for j in range(4 if DBG["v"] else 0):
                    t = c * 4 + j
                    ts = slice(j * 128, (j + 1) * 128)
                    pp = ptm[ntm % 2]
                    ntm += 1
                    for kc in range(8):
                        B.op("pe", lambda e, kc=kc, pp=pp: e.matmul(pp[:], lhsT=xcc[:, kc, ts], rhs=W[:, kc, 1024:1536],
                                                                    start=(kc == 0), stop=(kc == 7)), R=[W, xcc], W=[pp])
                    va = vau[t % 2]
                    B.op("act", lambda e, va=va, pp=pp: e.copy(va[:, :, 0:128], pp[:].rearrange("p (h e) -> p h e", h=NH)), R=[pp], W=[va])
                    B.dma(va_d[t], va[:], R=[va], W=[B.dr("va", t)])
                    if not DBG["zv"]:
                        continue
                    pp = ptm[ntm % 2]
                    ntm += 1
                    for kc in range(8):
                        B.op("pe", lambda e, kc=kc, pp=pp: e.matmul(pp[:], lhsT=xcc[:, kc, ts], rhs=W[:, kc, 2048:2560],
                                                                    start=(kc == 0), stop=(kc == 7)), R=[W, xcc], W=[pp])
                    z = zv[t % 2]
                    B.op("act", lambda e, z=z, pp=pp: e.activation(z[:], pp[:], AF.Gelu), R=[pp], W=[z])
                    vl = vln[t % 2]
                    if DBG["ln"]:
                        layernorm(ph, z, lng, lnb, z, 512, lns)
                    if not DBG["sgu"]:
                        continue
                    B.op("pool", lambda e, vl=vl, z=z: e.tensor_copy(vl[:], z[:]), R=[z], W=[vl])
                    for g in range(4):
                        B.op("pe", lambda e, g=g, vl=vl: e.matmul(psg[:, g * 128:(g + 1) * 128], lhsT=vl[:, g * 128:(g + 1) * 128],
                                                                  rhs=wsT[:, g, :], start=True, stop=True), R=[vl, wsT], W=[psg])
                    sv = svt[t % 2]
                    B.op("dve", lambda e, sv=sv: e.tensor_tensor(sv[:], psg[:], bsb[:], op=ALU.add), R=[psg, bsb], W=[sv])
                    B.op("dve", lambda e, sv=sv, sg=sg, ts=ts: e.tensor_tensor(sg[:, :, ts], sv[:].rearrange("p (g t) -> p g t", g=4),
                                                                                 uTc[:, :, ts], op=ALU.mult), R=[sv, uTc], W=[sg])
                if DBG["sgu"] and DBG["zv"] and DBG["v"]:
                    B.dma(sguT_d[:, :, cs].rearrange("g p t -> p g t"), sg[:], R=[sg], W=[B.dr("sguT", c)])
            B.barrier()

    def skip_blk(h, kb, qc):
        if qc * 4 <= kb < qc * 4 + 4:
            return False
        if kb < qc * 4:
            d = (qc * 4 - kb - 1) * 128 + 1
        else:
            d = (kb - qc * 4 - 4) * 128 + 1
        return SLOPES[h] * d > 45.0

    def phase_attn(l):
        lam_init = 0.8 - 0.6 * math.exp(-0.3 * l)
        with ExitStack() as ph:
            lt = B.sb(ph, [128, 4, 64], F32, "lt")
            B.dma(lt[:], lam_in[l:l + 1].rearrange("o a d -> o (a d)").to_broadcast([128, 256]).rearrange("p (a d) -> p a d", a=4), W=[lt])
            lj = B.sb(ph, [128, 2, 64], F32, "lj")
            ls2 = B.sb(ph, [128, 2], F32, "ls2")
            B.op("dve", lambda e: e.tensor_tensor(lj[:, 0, :], lt[:, 0, :], lt[:, 1, :], op=ALU.mult), R=[lt], W=[lj])
            B.op("dve", lambda e: e.tensor_tensor(lj[:, 1, :], lt[:, 2, :], lt[:, 3, :], op=ALU.mult), R=[lt, lj], W=[lj])
            B.op("dve", lambda e: e.tensor_reduce(ls2[:], lj[:], axis=AX.X, op=ALU.add), R=[lj], W=[ls2])
            B.op("act", lambda e: e.activation(ls2[:], ls2[:], AF.Exp), R=[ls2], W=[ls2])
            neglam = B.sb(ph, [128, 1], F32, "neglam")
            B.op("dve", lambda e: e.scalar_tensor_tensor(neglam[:], ls2[:, 1:2], -lam_init, ls2[:, 0:1], op0=ALU.add, op1=ALU.subtract),
                 R=[ls2], W=[neglam])
            gv = B.sb(ph, [128, 128], F32, "gv")
            B.dma(gv[:], att_norm_g[l:l + 1, :].to_broadcast([128, 128]), W=[gv])
            B.op("dve", lambda e: e.tensor_scalar(gv[:], gv[:], 1.0 - lam_init, None, op0=ALU.mult), R=[gv], W=[gv])
            negb = B.sb(ph, [128, NH, 128], F32, "negb")
            B.dma(negb[:], negb_in[:, :, :], W=[negb])
            KT = [[B.sb(ph, [70, Lc], BF16, "KT") for _ in range(2)] for _ in range(2)]
            V = B.sb(ph, [128, NT, 129], BF16, "V")
            Qc = [B.sb(ph, [70, 512], BF16, "Qc") for _ in range(4)]
            Pt = [B.sb(ph, [128, 512], BF16, "Pt") for _ in range(3)]
            pss = [B.ps(ph, [128, 512], F32, "pss") for _ in range(3)]
            pacc = [B.ps(ph, [128, 512], F32, "pacc") for _ in range(4)]
            psT = B.ps(ph, [128, 512], BF16, "psTb")
            o = [B.sb(ph, [128, 4, 128], F32, "o") for _ in range(2)]
            rz = B.sb(ph, [128, 1], F32, "rz")
            att = B.sb(ph, [128, 4, 128], F32, "att")
            sq = B.sb(ph, [128, 4, 128], F32, "sq")
            ss = B.sb(ph, [128, 4], F32, "ss")
            attb = B.sb(ph, [128, 4, 128], BF16, "attb")
            attT = [B.sb(ph, [128, 512], BF16, "attT") for _ in range(2)]
            nq = 0
            nsc = 0
            for h in range(NH):
                for comp in range(2):
                    for ver in range(2):
                        kt = KT[comp][ver]
                        B.dma(kt[0:64, :], kT_d[h, comp, :, :], R=[B.dr("qk1", (h, c)) for c in range(NCH)], W=[kt])
                        B.dma(kt[64:70, :], kaug_in[h, ver, :, :], W=[kt])
                B.dma(V[:], va_d[:, :, h, :].rearrange("n p e -> p n e"), R=[B.dr("va", t) for t in range(NT)], W=[V])
                for qc in range(NCH):
                    cs = slice(qc * 512, (qc + 1) * 512)
                    for comp in range(2):
                        q = Qc[nq % 4]
                        nq += 1
                        B.dma(q[0:64, :], qT_d[h, comp, :, cs], R=[B.dr("qk0", (h, qc))], W=[q])
                        B.dma(q[64:70, :], qaug_in[h, :, cs], W=[q])
                        kbs = [kb for kb in range(NT) if not skip_blk(h, kb, qc)]
                        def emit_score(idx, q=q, comp=comp, kbs=kbs, qc=qc, h=h, base=nsc):
                            kb = kbs[idx]
                            ks = slice(kb * 128, (kb + 1) * 128)
                            ps_ = pss[(base + idx) % 3]
                            if kb // 4 == qc:
                                for j in range(4):
                                    qb = qc * 4 + j
                                    js = slice(j * 128, (j + 1) * 128)
                                    if kb == qb:
                                        kt = KT[comp][0]
                                        B.op("pe", lambda e, kt=kt, ps_=ps_, js=js, ks=ks, q=q: e.matmul(ps_[:, js], lhsT=kt[0:64, ks], rhs=q[0:64, js],
                                                                                                       start=True, stop=True), R=[kt, q], W=[ps_])
                                    else:
                                        kt = KT[comp][0 if kb < qb else 1]
                                        B.op("pe", lambda e, kt=kt, ps_=ps_, js=js, ks=ks, q=q: e.matmul(ps_[:, js], lhsT=kt[0:70, ks], rhs=q[0:70, js],
                                                                                                       start=True, stop=True), R=[kt, q], W=[ps_])
                                jd = kb - qc * 4
                                jds = slice(jd * 128, (jd + 1) * 128)
                                B.op("dve", lambda e, ps_=ps_, jds=jds, h=h: e.tensor_tensor(ps_[:, jds], ps_[:, jds], negb[:, h, :], op=ALU.add),
                                     R=[ps_, negb], W=[ps_])
                            else:
                                kt = KT[comp][0 if kb < qc * 4 else 1]
                                B.op("pe", lambda e, kt=kt, ps_=ps_, ks=ks, q=q: e.matmul(ps_[:], lhsT=kt[0:70, ks], rhs=q[0:70, :],
                                                                                        start=True, stop=True), R=[kt, q], W=[ps_])

                        def emit_rest(idx, kbs=kbs, base=nsc):
                            kb = kbs[idx]
                            ps_ = pss[(base + idx) % 3]
                            P = Pt[(base + idx) % 3]
                            B.op("act", lambda e, P=P, ps_=ps_: e.activation(P[:], ps_[:], AF.Exp), R=[ps_], W=[P])
                            for j in range(4):
                                js = slice(j * 128, (j + 1) * 128)
                                B.op("pe", lambda e, j=j, js=js, P=P, kb=kb, idx=idx: e.matmul(pacc[j][:, 0:129], lhsT=P[:, js], rhs=V[:, kb, :],
                                                                                              start=(idx == 0), stop=(idx == len(kbs) - 1)),
                                     R=[P, V], W=[pacc[j]])

                        emit_score(0)
                        if len(kbs) > 1:
                            emit_score(1)
                        for idx in range(len(kbs)):
                            if idx + 2 < len(kbs):
                                emit_score(idx + 2)
                            emit_rest(idx)
                        nsc += len(kbs)
                        oc = o[comp]
                        for j in range(4):
                            B.op("dve", lambda e, j=j: e.reciprocal(rz[:], pacc[j][:, 128:129]), R=[pacc[j]], W=[rz])
                            B.op("dve", lambda e, j=j, oc=oc: e.tensor_scalar(oc[:, j, :], pacc[j][:, 0:128], rz[:, 0:1], None, op0=ALU.mult),
                                 R=[pacc[j], rz], W=[oc])
                    fl = lambda t_: t_[:].rearrange("p j e -> p (j e)")
                    B.op("dve", lambda e: e.scalar_tensor_tensor(fl(att), fl(o[1]), neglam[:, 0:1], fl(o[0]), op0=ALU.mult, op1=ALU.add),
                         R=[o[0], o[1], neglam], W=[att])
                    B.op("pool", lambda e: e.tensor_tensor(fl(sq), fl(att), fl(att), op=ALU.mult), R=[att], W=[sq])
                    B.op("dve", lambda e: e.tensor_reduce(ss[:], sq[:], axis=AX.X, op=ALU.add), R=[sq], W=[ss])
                    B.op("act", lambda e: e.activation(ss[:], ss[:], AF.Sqrt, bias=eps_t[:, 0:1], scale=1.0 / 128.0), R=[ss, eps_t], W=[ss])
                    B.op("dve", lambda e: e.reciprocal(ss[:], ss[:]), R=[ss], W=[ss])
                    B.op("dve", lambda e: e.tensor_tensor(att[:], att[:], ss[:].unsqueeze(2).to_broadcast([128, 4, 128]), op=ALU.mult),
                         R=[att, ss], W=[att])
                    B.op("dve", lambda e: e.tensor_tensor(attb[:], att[:], gv[:].unsqueeze(1).to_broadcast([128, 4, 128]), op=ALU.mult),
                         R=[att, gv], W=[attb])
                    for j in range(4):
                        B.op("pe", lambda e, j=j: e.transpose(psT[:, j * 128:(j + 1) * 128], attb[:, j, :], identb[:]), R=[attb, identb], W=[psT])
                    at = attT[qc % 2]
                    B.op("act", lambda e, at=at: e.copy(at[:], psT[:]), R=[psT], W=[at])
                    B.dma(attT_d[h, :, cs], at[:], R=[at], W=[B.dr("attT", (h, qc))])
            B.barrier()

    TWO_PI = 2.0 * math.pi
    SC_S = TWO_PI * (1.0 - 1e-6)
    SC_B = -math.pi * (1.0 - 1e-6)

    def sincos(a, n, sout, cout, scr):
        r, ri, rf, fx = scr[0:4]
        for off, (ot, oap) in ((0.5, sout), (0.75, cout)):
            B.op("dve", lambda e, off=off: e.tensor_scalar(r[:, 0:n], a, 1.0 / TWO_PI, off, op0=ALU.mult, op1=ALU.add), R=scr[4], W=[r])
            B.op("dve", lambda e: e.tensor_copy(ri[:, 0:n], r[:, 0:n]), R=[r], W=[ri])
            B.op("dve", lambda e: e.tensor_copy(rf[:, 0:n], ri[:, 0:n]), R=[ri], W=[rf])
            B.op("dve", lambda e: e.tensor_tensor(fx[:, 0:n], rf[:, 0:n], r[:, 0:n], op=ALU.is_gt), R=[rf, r], W=[fx])
            B.op("dve", lambda e: e.tensor_tensor(rf[:, 0:n], rf[:, 0:n], fx[:, 0:n], op=ALU.subtract), R=[rf, fx], W=[rf])
            B.op("dve", lambda e: e.tensor_tensor(r[:, 0:n], r[:, 0:n], rf[:, 0:n], op=ALU.subtract), R=[r, rf], W=[r])
            B.op("act", lambda e, oap=oap: e.activation(oap, r[:, 0:n], AF.Sin, bias=sinb[:, 0:1], scale=SC_S), R=[r, sinb], W=[ot])

    sinb = B.sb(glob, [128, 1], F32, "sinb")
    B.op("dve", lambda e: e.memset(sinb[:], SC_B), W=[sinb])
    iot_s = B.sb(glob, [128, 512], F32, "iot_s")
    B.op("pool", lambda e: e.iota(iot_s[:], pattern=[[1, 512]], base=0, channel_multiplier=0,
                                  allow_small_or_imprecise_dtypes=True), W=[iot_s])
    bmask = B.sb(glob, [128, 2], F32, "bmask")
    B.dma(bmask[:], bmask_in[:, :], W=[bmask])

    def phase_s5(l):
        with ExitStack() as ph:
            mk = lambda shp, dt=F32, nm="t": B.sb(ph, shp, dt, nm)
            scr_t = [mk([128, 512]), mk([128, 512], I32), mk([128, 512]), mk([128, 512])]
            sp = mk([128, 3, 32])
            B.dma(sp[:], s5p_in[l], W=[sp])
            dt_ = mk([128, 32]); rho = mk([128, 32]); th = mk([128, 32]); thT = mk([128, 32])
            cT = mk([128, 32]); sT = mk([128, 32])
            B.op("act", lambda e: e.activation(dt_[:], sp[:, 2, :], AF.Exp), R=[sp], W=[dt_])
            B.op("dve", lambda e: e.tensor_tensor(rho[:], sp[:, 0, :], dt_[:], op=ALU.mult), R=[sp, dt_], W=[rho])
            B.op("act", lambda e: e.activation(rho[:], rho[:], AF.Exp), R=[rho], W=[rho])
            B.op("dve", lambda e: e.tensor_tensor(th[:], sp[:, 1, :], dt_[:], op=ALU.mult), R=[sp, dt_], W=[th])
            B.op("dve", lambda e: e.tensor_scalar(thT[:], th[:], float(S5T), None, op0=ALU.mult), R=[th], W=[thT])
            sincos(thT[:], 32, (sT, sT[:]), (cT, cT[:]), scr_t + [[thT]])
            sq = mk([128, 3, 512]); bq = mk([128, 2, 512])
            B.dma(sq[:], s5q_in[l], W=[sq])
            B.dma(bq[:], s5b_in[l], W=[bq])
            dq = mk([128, 512]); mag = mk([128, 512]); tq = mk([128, 512]); sn = mk([128, 512]); cs_ = mk([128, 512])
            B.op("act", lambda e: e.activation(dq[:], sq[:, 2, :], AF.Exp), R=[sq], W=[dq])
            B.op("dve", lambda e: e.tensor_tensor(mag[:], sq[:, 0, :], dq[:], op=ALU.mult), R=[sq, dq], W=[mag])
            B.op("act", lambda e: e.activation(mag[:], mag[:], AF.Exp), R=[mag], W=[mag])
            B.op("dve", lambda e: e.tensor_tensor(tq[:], sq[:, 1, :], dq[:], op=ALU.mult), R=[sq, dq], W=[tq])
            sincos(tq[:], 512, (sn, sn[:]), (cs_, cs_[:]), scr_t + [[tq]])
            abr = mk([128, 512]); abi = mk([128, 512]); den = mk([128, 512]); t1 = mk([128, 512]); t2 = mk([128, 512])
            fr = mk([128, 512]); fi = mk([128, 512])
            tt = lambda o_, a_, b_, op_, R_, W_: B.op("dve", lambda e: e.tensor_tensor(o_, a_, b_, op=op_), R=R_, W=W_)
            tt(abr[:], mag[:], cs_[:], ALU.mult, [mag, cs_], [abr])
            tt(abi[:], mag[:], sn[:], ALU.mult, [mag, sn], [abi])
            tt(den[:], sq[:, 0, :], sq[:, 0, :], ALU.mult, [sq], [den])
            tt(t1[:], sq[:, 1, :], sq[:, 1, :], ALU.mult, [sq], [t1])
            tt(den[:], den[:], t1[:], ALU.add, [den, t1], [den])
            B.op("dve", lambda e: e.reciprocal(den[:], den[:]), R=[den], W=[den])
            B.op("dve", lambda e: e.tensor_scalar(abr[:], abr[:], -1.0, None, op0=ALU.add), R=[abr], W=[abr])
            tt(t1[:], abr[:], sq[:, 0, :], ALU.mult, [abr, sq], [t1])
            tt(t2[:], abi[:], sq[:, 1, :], ALU.mult, [abi, sq], [t2])
            tt(fr[:], t1[:], t2[:], ALU.add, [t1, t2], [fr])
            tt(fr[:], fr[:], den[:], ALU.mult, [fr, den], [fr])
            tt(t1[:], abi[:], sq[:, 0, :], ALU.mult, [abi, sq], [t1])
            tt(t2[:], abr[:], sq[:, 1, :], ALU.mult, [abr, sq], [t2])
            tt(fi[:], t1[:], t2[:], ALU.subtract, [t1, t2], [fi])
            tt(fi[:], fi[:], den[:], ALU.mult, [fi, den], [fi])
            bbr = abr; bbi = abi
            tt(t1[:], fr[:], bq[:, 0, :], ALU.mult, [fr, bq], [t1])
            tt(t2[:], fi[:], bq[:, 1, :], ALU.mult, [fi, bq], [t2])
            tt(bbr[:], t1[:], t2[:], ALU.subtract, [t1, t2], [bbr])
            tt(t1[:], fr[:], bq[:, 1, :], ALU.mult, [fr, bq], [t1])
            tt(t2[:], fi[:], bq[:, 0, :], ALU.mult, [fi, bq], [t2])
            tt(bbi[:], t1[:], t2[:], ALU.add, [t1, t2], [bbi])
            BbT = mk([128, 8, 2, 2, 64], BF16)
            for ri, src in ((0, bbr), (1, bbi)):
                for gl in range(2):
                    B.op("dve", lambda e, ri=ri, gl=gl, src=src: e.tensor_scalar(BbT[:, :, ri, gl, :], src[:].rearrange("p (a n) -> p a n", a=8),
                                                                                bmask[:, gl:gl + 1], None, op0=ALU.mult), R=[src, bmask], W=[BbT])
            cf = mk([128, 2, 32, 32]); CT = mk([128, 2, 32, 32], BF16)
            B.dma(cf[:], s5c_in[l], W=[cf])
            B.op("dve", lambda e: e.tensor_copy(CT[:, 0], cf[:, 0]), R=[cf], W=[CT])
            B.op("dve", lambda e: e.tensor_scalar(CT[:, 1], cf[:, 1], -1.0, None, op0=ALU.mult), R=[cf, CT], W=[CT])
            us = [mk([128, Lc], BF16, "us") for _ in range(2)]
            Bls = [mk([32, 2, 2, 128], BF16, "Bl") for _ in range(2)]
            XF = [mk([128, Lc], BF16, "XF") for _ in range(2)]
            tab = [[mk([128, 512]) for _ in range(2)] for _ in range(2)]
            rhob = [mk([128, 512]) for _ in range(2)]
            phs = mk([128, 512])
            pB = [[B.ps(ph, [128, 512], F32, "pB") for _ in range(2)] for _ in range(2)]
            py = [B.ps(ph, [128, 512], F32, "py") for _ in range(2)]
            w1 = [mk([128, 512]) for _ in range(4)]
            bp = [[mk([128, 512]) for _ in range(2)] for _ in range(2)]
            z = [[mk([128, 512]) for _ in range(2)] for _ in range(2)]
            xb = [[mk([128, 512], BF16) for _ in range(2)] for _ in range(2)]
            ini = [[mk([128, 1]) for _ in range(2)] for _ in range(2)]
            i1 = mk([128, 1]); i2 = mk([128, 1])
            yo = [mk([32, 512]) for _ in range(2)]
            n = 0
            for gp in range(16):
                kc = gp // 4
                base = (gp % 4) * 32
                u = us[gp % 2]
                B.dma(u[0:32, :], usb_d[kc, base:base + 32, :], R=[B.dr("usb", (kc, c)) for c in range(NCH)], W=[u])
                Bl = Bls[gp % 2]
                B.dma(Bl[:], BbT[base:base + 32, kc:8:4].rearrange("p d r a n -> p d r (a n)"), R=[BbT], W=[Bl])
                for d in range(2):
                    col = d * 16 + gp
                    B.op("dve", lambda e, col=col: e.tensor_scalar(phs[:], iot_s[:], th[:, col:col + 1], None, op0=ALU.mult), R=[iot_s, th], W=[phs])
                    sincos(phs[:], 512, (tab[d][1], tab[d][1][:]), (tab[d][0], tab[d][0][:]), scr_t + [[phs]])
                    B.op("dve", lambda e, col=col, d=d: e.tensor_scalar(rhob[d][:], iot_s[:], 0.0, rho[:, col:col + 1], op0=ALU.mult, op1=ALU.add),
                         R=[iot_s, rho], W=[rhob[d]])
                for d in range(2):
                    col = d * 16 + gp
                    rev = (d == 1)
                    V_ = (lambda ap: ap[:, ::-1]) if rev else (lambda ap: ap)
                    tc, ts_ = tab[d][0], tab[d][1]
                    order = range(NCH) if d == 0 else range(NCH - 1, -1, -1)
                    prev = None
                    for c in order:
                        cs = slice(c * 512, (c + 1) * 512)
                        pr, pi = pB[n % 2]
                        bpr, bpi = bp[n % 2]
                        zr, zi = z[n % 2]
                        for ri, pp in ((0, pr), (1, pi)):
                            B.op("pe", lambda e, ri=ri, pp=pp, d=d: e.matmul(pp[:], lhsT=Bl[0:32, d, ri, :],
                                                                           rhs=u[0:32, cs], start=True, stop=True), R=[Bl, u], W=[pp])
                        tcv, tsv = V_(tc[:]), V_(ts_[:])
                        tt(w1[0][:], tcv, pr[:], ALU.mult, [tc, pr], [w1[0]])
                        tt(w1[1][:], tsv, pi[:], ALU.mult, [ts_, pi], [w1[1]])
                        tt(w1[2][:], tcv, pi[:], ALU.mult, [tc, pi], [w1[2]])
                        tt(w1[3][:], tsv, pr[:], ALU.mult, [ts_, pr], [w1[3]])
                        B.op("pool", lambda e, bpr=bpr: e.tensor_tensor(bpr[:], w1[0][:], w1[1][:], op=ALU.add), R=[w1[0], w1[1]], W=[bpr])
                        B.op("pool", lambda e, bpi=bpi: e.tensor_tensor(bpi[:], w1[2][:], w1[3][:], op=ALU.subtract), R=[w1[2], w1[3]], W=[bpi])
                        if prev is None:
                            i_r = i_i = 0.0
                            Ri = []
                        else:
                            pzr, pzi = prev
                            lc_ = slice(0, 1) if rev else slice(511, 512)
                            ir, ii = ini[n % 2]
                            B.op("dve", lambda e, pzi=pzi, lc_=lc_: e.tensor_tensor(i1[:], sT[:, col:col + 1], pzi[:, lc_], op=ALU.mult), R=[sT, pzi], W=[i1])
                            B.op("dve", lambda e, pzr=pzr, lc_=lc_: e.tensor_tensor(i2[:], sT[:, col:col + 1], pzr[:, lc_], op=ALU.mult), R=[sT, pzr], W=[i2])
                            B.op("dve", lambda e, pzr=pzr, lc_=lc_, ir=ir: e.scalar_tensor_tensor(ir[:], pzr[:, lc_], cT[:, col:col + 1], i1[:], op0=ALU.mult, op1=ALU.subtract),
                                 R=[cT, pzr, i1], W=[ir])
                            B.op("dve", lambda e, pzi=pzi, lc_=lc_, ii=ii: e.scalar_tensor_tensor(ii[:], pzi[:, lc_], cT[:, col:col + 1], i2[:], op0=ALU.mult, op1=ALU.add),
                                 R=[cT, pzi, i2], W=[ii])
                            bnd = (c == NCH // 2) if d == 0 else (c == NCH // 2 - 1)
                            if bnd:
                                B.op("dve", lambda e, ir=ir: e.tensor_tensor(ir[:], ir[:], gate[:], op=ALU.mult), R=[ir, gate], W=[ir])
                                B.op("dve", lambda e, ii=ii: e.tensor_tensor(ii[:], ii[:], gate[:], op=ALU.mult), R=[ii, gate], W=[ii])
                            i_r, i_i = ir[:, 0:1], ii[:, 0:1]
                            Ri = [ir, ii]
                        rb = rhob[d]
                        B.op("dve", lambda e, zr=zr, bpr=bpr, i_r=i_r: e.tensor_tensor_scan(V_(zr[:]), V_(rb[:]), V_(bpr[:]), i_r, op0=ALU.mult, op1=ALU.add),
                             R=[rb, bpr] + Ri, W=[zr])
                        B.op("dve", lambda e, zi=zi, bpi=bpi, i_i=i_i: e.tensor_tensor_scan(V_(zi[:]), V_(rb[:]), V_(bpi[:]), i_i, op0=ALU.mult, op1=ALU.add),
                             R=[rb, bpi] + Ri, W=[zi])
                        prev = (zr, zi)
                        if d == 0:
                            oxr, oxi = XF[0][:, cs], XF[1][:, cs]
                            Wx = [XF[0]], [XF[1]]
                        else:
                            oxr, oxi = xb[n % 2][0][:], xb[n % 2][1][:]
                            Wx = [xb[n % 2][0]], [xb[n % 2][1]]
                        B.op("pool", lambda e, zr=zr: e.tensor_tensor(w1[0][:], tcv, zr[:], op=ALU.mult), R=[tc, zr], W=[w1[0]])
                        B.op("pool", lambda e, zi=zi: e.tensor_tensor(w1[1][:], tsv, zi[:], op=ALU.mult), R=[ts_, zi], W=[w1[1]])
                        B.op("pool", lambda e, zi=zi: e.tensor_tensor(w1[2][:], tcv, zi[:], op=ALU.mult), R=[tc, zi], W=[w1[2]])
                        B.op("pool", lambda e, zr=zr: e.tensor_tensor(w1[3][:], tsv, zr[:], op=ALU.mult), R=[ts_, zr], W=[w1[3]])
                        tt(oxr, w1[0][:], w1[1][:], ALU.subtract, [w1[0], w1[1]], Wx[0])
                        tt(oxi, w1[2][:], w1[3][:], ALU.add, [w1[2], w1[3]], Wx[1])
                        if d == 1:
                            pyy = py[n % 2]
                            ops = ((0, 0, XF[0], XF[0][:, cs]), (1, 0, XF[1], XF[1][:, cs]), (0, 1, xb[n % 2][0], xb[n % 2][0][:]), (1, 1, xb[n % 2][1], xb[n % 2][1][:]))
                            for k_, (ri, dd, srcT, srcap) in enumerate(ops):
                                B.op("pe", lambda e, ri=ri, dd=dd, srcap=srcap, k_=k_, pyy=pyy: e.matmul(pyy[0:32, :], lhsT=CT[:, ri, dd * 16 + gp, :], rhs=srcap,
                                                                                                      start=(k_ == 0), stop=(k_ == 3)), R=[CT, srcT], W=[pyy])
                            y_ = yo[n % 2]
                            B.op("act", lambda e, y_=y_, pyy=pyy: e.copy(y_[:], pyy[0:32, :]), R=[pyy], W=[y_])
                            B.dma(yT_d[gp, :, cs], y_[:], R=[y_], W=[B.dr("yT", (gp, c))])
                        n += 1
            B.barrier()
        with ExitStack() as ph:
            mk = lambda shp, dt=F32, nm="t": B.sb(ph, shp, dt, nm)
            gw = mk([128, 4, 512], BF16)
            stage = [mk([128, 512]) for _ in range(2)]
            load_w_bf16(ph, gw, lambda kc: s5_glu_w[l, kc * 128:(kc + 1) * 128, :], 512, stage, kcs=4)
            glub = mk([128, 4]); dsk = mk([128, 4])
            B.dma(glub[:], s5_glu_b[l], W=[glub])
            B.dma(dsk[:], s5_d[l], W=[dsk])
            yt = [mk([128, 4, 512]) for _ in range(2)]
            uf = [mk([128, 4, 512]) for _ in range(2)]
            y2f = [mk([128, 4, 512]) for _ in range(2)]
            y2b = [mk([128, 4, 512], BF16) for _ in range(2)]
            sg = [mk([128, 512]) for _ in range(2)]
            so = [mk([128, 4, 512], BF16) for _ in range(2)]
            pg = [B.ps(ph, [128, 512], F32, "pg") for _ in range(2)]
            n = 0
            for c in range(NCH):
                cs = slice(c * 512, (c + 1) * 512)
                y = yt[c % 2]; u = uf[c % 2]; yf = y2f[c % 2]; yb = y2b[c % 2]; s_o = so[c % 2]
                B.dma(y[:], yT_d[:, :, cs].rearrange("(kc q) r t -> (q r) kc t", q=4), R=[B.dr("yT", (gp, c)) for gp in range(16)], W=[y])
                B.dma(u[:], usf_d[:, :, cs].rearrange("g p t -> p g t"), R=[B.dr("usf", (g, c)) for g in range(4)], W=[u])
                for kc in range(4):
                    B.op("dve", lambda e, kc=kc: e.scalar_tensor_tensor(y[:, kc, :], u[:, kc, :], dsk[:, kc:kc + 1], y[:, kc, :], op0=ALU.mult, op1=ALU.add),
                         R=[u, dsk, y], W=[y])
                B.op("act", lambda e: e.activation(yf[:], y[:], AF.Gelu), R=[y], W=[yf])
                B.op("pool", lambda e: e.tensor_copy(yb[:], yf[:]), R=[yf], W=[yb])
                for fb in range(4):
                    pp = pg[n % 2]; s_ = sg[n % 2]
                    n += 1
                    for kc in range(4):
                        B.op("pe", lambda e, kc=kc, fb=fb, pp=pp: e.matmul(pp[:], lhsT=gw[:, kc, fb * 128:(fb + 1) * 128], rhs=yb[:, kc, :],
                                                                          start=(kc == 0), stop=(kc == 3)), R=[gw, yb], W=[pp])
                    B.op("act", lambda e, fb=fb, pp=pp, s_=s_: e.activation(s_[:], pp[:], AF.Sigmoid, bias=glub[:, fb:fb + 1], scale=1.0), R=[pp, glub], W=[s_])
                    B.op("dve", lambda e, fb=fb, s_=s_: e.tensor_tensor(s_o[:, fb, :], s_[:], yf[:, fb, :], op=ALU.mult), R=[s_, yf], W=[s_o])
                B.dma(s5T_d[:, :, cs].rearrange("g p t -> p g t"), s_o[:], R=[s_o], W=[B.dr("s5T", c)])
            B.barrier()

    def phase_merge(l, x_src, x_key):
        with ExitStack() as ph:
            mk = lambda shp, dt=F32, nm="t": B.sb(ph, shp, dt, nm)
            Wg = mk([128, 8, 3072], BF16)
            Wb = mk([128, 12, 1024], BF16)
            Wo = mk([128, 8, 1024], BF16)
            stage = [mk([128, 1024]) for _ in range(2)]
            load_w_bf16(ph, Wg, lambda kc: w_in[l, kc * 128:(kc + 1) * 128, 3072:6144], 3072, stage)
            load_w_bf16(ph, Wb, lambda a: w_branch[l, a // 4, (a % 4) * 128:(a % 4 + 1) * 128, :], 1024, stage, kcs=12)
            load_w_bf16(ph, Wo, lambda kc: w_o[l, kc * 128:(kc + 1) * 128, :], 1024, stage)
            bg = mk([128, 3, 8])
            B.dma(bg[:], b_gate[l], W=[bg])
            g1 = mk([128, D]); b1 = mk([128, D])
            B.dma(g1[:], ln1_g[l:l + 1, :].to_broadcast([128, D]), W=[g1])
            B.dma(b1[:], ln1_b[l:l + 1, :].to_broadcast([128, D]), W=[b1])
            lns = ln_scratch(ph)
            xc = [mk([128, 8, 512], BF16) for _ in range(2)]
            br = [[mk([128, 4, 512], BF16) for _ in range(3)] for _ in range(2)]
            mT = mk([128, 8, 512], BF16)
            gt = [mk([128, 512]) for _ in range(2)]
            acc = mk([128, 512]); tmp = [mk([128, 512]) for _ in range(2)]
            pgt = [B.ps(ph, [128, 512], F32, "pgt") for _ in range(2)]
            ppj = [B.ps(ph, [128, 512], F32, "ppj") for _ in range(2)]
            po = [B.ps(ph, [128, 512], F32, "po") for _ in range(2)]
            psT = B.ps(ph, [128, 1024], F32, "psT")
            xt = [mk([128, D]) for _ in range(2)]
            rr = [mk([128, D]) for _ in range(2)]
            x1c = [mk([128, 8, 512], BF16) for _ in range(2)]
            srcs = ((attT_d, "attT", True), (sguT_d, "sguT", False), (s5T_d, "s5T", False))
            n = 0
            for c in range(NCH):
                cs = slice(c * 512, (c + 1) * 512)
                xcc = xc[c % 2]
                B.dma(xcc[:], xT_d[:, :, cs].rearrange("k p t -> p k t"), R=[B.dr("xT", c)], W=[xcc])
                brc = br[c % 2]
                for i, (dd, nm, perh) in enumerate(srcs):
                    Rk = [B.dr(nm, (h, c)) for h in range(4)] if perh else [B.dr(nm, c)]
                    B.dma(brc[i][:], dd[:, :, cs].rearrange("g p t -> p g t"), R=Rk, W=[brc[i]])
                for db in range(8):
                    for i in range(3):
                        pg_ = pgt[n % 2]; pp = ppj[n % 2]; g_ = gt[n % 2]; t_ = tmp[n % 2]
                        n += 1
                        c0 = i * 1024 + db * 128
                        for kc in range(8):
                            B.op("pe", lambda e, kc=kc, c0=c0, pg_=pg_: e.matmul(pg_[:], lhsT=Wg[:, kc, c0:c0 + 128], rhs=xcc[:, kc, :],
                                                                              start=(kc == 0), stop=(kc == 7)), R=[Wg, xcc], W=[pg_])
                        B.op("act", lambda e, i=i, db=db, pg_=pg_, g_=g_: e.activation(g_[:], pg_[:], AF.Sigmoid, bias=bg[:, i, db:db + 1], scale=1.0),
                             R=[pg_, bg], W=[g_])
                        for kc in range(4):
                            B.op("pe", lambda e, kc=kc, i=i, db=db, pp=pp: e.matmul(pp[:], lhsT=Wb[:, i * 4 + kc, db * 128:(db + 1) * 128], rhs=brc[i][:, kc, :],
                                                                                 start=(kc == 0), stop=(kc == 3)), R=[Wb, brc[i]], W=[pp])
                        if i == 0:
                            B.op("dve", lambda e, pp=pp, g_=g_: e.tensor_tensor(acc[:], g_[:], pp[:], op=ALU.mult), R=[g_, pp], W=[acc])
                        else:
                            B.op("dve", lambda e, pp=pp, g_=g_, t_=t_: e.tensor_tensor(t_[:], g_[:], pp[:], op=ALU.mult), R=[g_, pp], W=[t_])
                            if i == 1:
                                B.op("pool", lambda e, t_=t_: e.tensor_tensor(acc[:], acc[:], t_[:], op=ALU.add), R=[acc, t_], W=[acc])
                            else:
                                B.op("pool", lambda e, t_=t_, db=db: e.tensor_tensor(mT[:, db, :], acc[:], t_[:], op=ALU.add), R=[acc, t_], W=[mT])
                x1cc = x1c[c % 2]
                for j in range(4):
                    t = c * 4 + j
                    x_ = xt[t % 2]; r_ = rr[t % 2]
                    B.dma(x_[:], x_src[t * 128:(t + 1) * 128, :], R=([B.dr(x_key, t)] if x_key else []), W=[x_])
                    for hf in range(2):
                        p_ = po[hf]
                        for kc in range(8):
                            B.op("pe", lambda e, kc=kc, hf=hf, p_=p_, j=j: e.matmul(p_[:], lhsT=mT[:, kc, j * 128:(j + 1) * 128], rhs=Wo[:, kc, hf * 512:(hf + 1) * 512],
                                                                                 start=(kc == 0), stop=(kc == 7)), R=[mT, Wo], W=[p_])
                        B.op("dve", lambda e, hf=hf, p_=p_, x_=x_, r_=r_: e.scalar_tensor_tensor(r_[:, hf * 512:(hf + 1) * 512], x_[:, hf * 512:(hf + 1) * 512], ALPHA, p_[:],
                                                                                               op0=ALU.mult, op1=ALU.add), R=[x_, p_], W=[r_])
                    layernorm(ph, r_, g1, b1, r_, D, lns)
                    B.dma(x1_d[t * 128:(t + 1) * 128, :], r_[:], R=[r_], W=[B.dr("x1", t)])
                    to_fm(ph, r_, psT, x1cc, j)
                B.dma(x1T_d[:, :, cs].rearrange("k p t -> p k t"), x1cc[:], R=[x1cc], W=[B.dr("x1T", c)])
            B.barrier()

    iot16 = B.sb(glob, [128, 16], F32, "iot16")
    B.op("pool", lambda e: e.iota(iot16[:], pattern=[[1, 16]], base=0, channel_multiplier=0, allow_small_or_imprecise_dtypes=True), W=[iot16])

    def phase_peer(l, last):
        puv = peer_uv.rearrange("l e d -> (l e) d")
        with ExitStack() as ph:
            mk = lambda shp, dt=F32, nm="t": B.sb(ph, shp, dt, nm)
            Wq = mk([128, 8, 2048], BF16)
            keysT = mk([128, 16, 128], BF16)
            pS = [B.ps(ph, [128, 512], F32, "pS") for _ in range(4)]
            pq = [B.ps(ph, [128, 512], F32, "pq") for _ in range(2)]
            psT = B.ps(ph, [128, 1024], F32, "psT")
            with ExitStack() as ph0:
                stage = [B.sb(ph0, [128, 1024], F32, "stg") for _ in range(2)]
                load_w_bf16(ph0, Wq, lambda kc: peer_w_q[l, kc * 128:(kc + 1) * 128, :], 2048, stage)
                kf = B.sb(ph0, [128, 16, 128], F32, "kf")
                B.dma(kf[:], peer_keys[l].rearrange("a k d -> k a d"), W=[kf])
                for a in range(16):
                    B.op("pe", lambda e, a=a: e.transpose(pS[a // 4][:, (a % 4) * 128:(a % 4 + 1) * 128], kf[:, a, :], ident[:]), R=[kf, ident], W=[pS[a // 4]])
                for b_ in range(4):
                    B.op("act", lambda e, b_=b_: e.copy(keysT[:, b_ * 4:(b_ + 1) * 4, :].rearrange("p a k -> p (a k)"), pS[b_][:]), R=[pS[b_]], W=[keysT])
                B.barrier()
            g2 = mk([128, D]); b2 = mk([128, D])
            B.dma(g2[:], ln2_g[l:l + 1, :].to_broadcast([128, D]), W=[g2])
            B.dma(b2[:], ln2_b[l:l + 1, :].to_broadcast([128, D]), W=[b2])
            lns = ln_scratch(ph)
            x1c = [mk([128, 8, 512], BF16) for _ in range(2)]
            qT = mk([128, 16, 512], BF16)
            xt = [mk([128, D]) for _ in range(2)]
            S = mk([128, 16, 128]); S2 = mk([128, 16, 128])
            v8 = mk([128, 16, 16]); i8 = mk([128, 16, 16], U32); sif = mk([128, 16, 16])
            cand = mk([128, 8, 16, 16]); cand2 = mk([128, 8, 256])
            sc = mk([128, 8, 16]); fl = mk([128, 8, 16], U32); fi_ = mk([128, 8, 16], U32); fj_ = mk([128, 8, 16], U32)
            fif = mk([128, 8, 16]); fjf = mk([128, 8, 16])
            oh = mk([128, 8, 16, 16]); e1 = mk([128, 8, 16]); e2 = mk([128, 8, 16])
            eid = mk([128, 128], U32)
            ex = mk([128, 8, 16]); zz = mk([128, 8]); gg = mk([128, 128])
            NUG = 10
            Ug = [mk([128, 2 * D], BF16, "Ug") for _ in range(NUG)]
            Dg = [mk([128, 128], BF16, "Dg") for _ in range(3)]
            hs = [mk([128, 1], F32, "hs") for _ in range(4)]
            ws = [mk([128, 1], F32, "ws") for _ in range(4)]
            rr = [mk([128, D]) for _ in range(2)]
            EMAX = nc.gpsimd.to_reg(DEPTH * nexp - 1)
            x2c = [mk([128, 8, 512], BF16) for _ in range(1)]
            ng = 0
            for c in range(NCH):
                cs = slice(c * 512, (c + 1) * 512)
                xcc = x1c[c % 2]
                B.dma(xcc[:], x1T_d[:, :, cs].rearrange("k p t -> p k t"), R=[B.dr("x1T", c)], W=[xcc])
                for a in range(16):
                    pp = pq[a % 2]
                    for kc in range(8):
                        B.op("pe", lambda e, kc=kc, a=a, pp=pp: e.matmul(pp[:], lhsT=Wq[:, kc, a * 128:(a + 1) * 128], rhs=xcc[:, kc, :],
                                                                        start=(kc == 0), stop=(kc == 7)), R=[Wq, xcc], W=[pp])
                    B.op("act", lambda e, a=a, pp=pp: e.copy(qT[:, a, :], pp[:]), R=[pp], W=[qT])
                x2cc = x2c[0]
                for j in range(4):
                    t = c * 4 + j
                    js = slice(j * 128, (j + 1) * 128)
                    x_ = xt[t % 2]
                    B.dma(x_[:], x1_d[t * 128:(t + 1) * 128, :], R=[B.dr("x1", t)], W=[x_])
                    for a in range(16):
                        B.op("pe", lambda e, a=a: e.matmul(pS[a // 4][:, (a % 4) * 128:(a % 4 + 1) * 128], lhsT=qT[:, a, js], rhs=keysT[:, a, :],
                                                          start=True, stop=True), R=[qT, keysT], W=[pS[a // 4]])
                    for b_ in range(4):
                        B.op("act", lambda e, b_=b_: e.copy(S[:, b_ * 4:(b_ + 1) * 4, :].rearrange("p a k -> p (a k)"), pS[b_][:]), R=[pS[b_]], W=[S])
                    for a in range(16):
                        B.op("dve", lambda e, a=a: e.max(out=v8[:, a, 0:8], in_=S[:, a, :]), R=[S], W=[v8])
                        B.op("dve", lambda e, a=a: e.max_index(i8[:, a, 0:8], v8[:, a, 0:8], S[:, a, :]), R=[S, v8], W=[i8])
                        B.op("dve", lambda e, a=a: e.match_replace(out=S2[:, a, :], in_to_replace=v8[:, a, 0:8], in_values=S[:, a, :], imm_value=NEG),
                             R=[S, v8], W=[S2])
                        B.op("dve", lambda e, a=a: e.max(out=v8[:, a, 8:16], in_=S2[:, a, :]), R=[S2], W=[v8])
                        B.op("dve", lambda e, a=a: e.max_index(i8[:, a, 8:16], v8[:, a, 8:16], S2[:, a, :]), R=[S2, v8], W=[i8])
                    B.op("dve", lambda e: e.tensor_copy(sif[:], i8[:]), R=[i8], W=[sif])
                    v8v = v8[:].rearrange("p (h c) k -> p h c k", c=2)
                    siv = sif[:].rearrange("p (h c) k -> p h c k", c=2)
                    B.op("dve", lambda e: e.tensor_tensor(cand[:], v8v[:, :, 0, :].unsqueeze(3).to_broadcast([128, 8, 16, 16]),
                                                          v8v[:, :, 1, :].unsqueeze(2).to_broadcast([128, 8, 16, 16]), op=ALU.add), R=[v8], W=[cand])
                    candf = cand[:].rearrange("p h i j -> p h (i j)")
                    for h in range(8):
                        B.op("dve", lambda e, h=h: e.max(out=sc[:, h, 0:8], in_=candf[:, h, :]), R=[cand], W=[sc])
                        B.op("dve", lambda e, h=h: e.max_index(fl[:, h, 0:8], sc[:, h, 0:8], candf[:, h, :]), R=[cand, sc], W=[fl])
                        B.op("dve", lambda e, h=h: e.match_replace(out=cand2[:, h, :], in_to_replace=sc[:, h, 0:8], in_values=candf[:, h, :], imm_value=NEG),
                             R=[cand, sc], W=[cand2])
                        B.op("dve", lambda e, h=h: e.max(out=sc[:, h, 8:16], in_=cand2[:, h, :]), R=[cand2], W=[sc])
                        B.op("dve", lambda e, h=h: e.max_index(fl[:, h, 8:16], sc[:, h, 8:16], cand2[:, h, :]), R=[cand2, sc], W=[fl])
                    B.op("dve", lambda e: e.tensor_single_scalar(fi_[:], fl[:], 4, op=ALU.logical_shift_right), R=[fl], W=[fi_])
                    B.op("dve", lambda e: e.tensor_single_scalar(fj_[:], fl[:], 15, op=ALU.bitwise_and), R=[fl], W=[fj_])
                    B.op("dve", lambda e: e.tensor_copy(fif[:], fi_[:]), R=[fi_], W=[fif])
                    B.op("dve", lambda e: e.tensor_copy(fjf[:], fj_[:]), R=[fj_], W=[fjf])
                    io_b = iot16[:].unsqueeze(1).unsqueeze(1).to_broadcast([128, 8, 16, 16])
                    for (ff, cc_, eo) in ((fif, 0, e1), (fjf, 1, e2)):
                        B.op("dve", lambda e, ff=ff: e.tensor_tensor(oh[:], io_b, ff[:].unsqueeze(3).to_broadcast([128, 8, 16, 16]), op=ALU.is_equal),
                             R=[iot16, ff], W=[oh])
                        B.op("dve", lambda e, cc_=cc_: e.tensor_tensor(oh[:], oh[:], siv[:, :, cc_, :].unsqueeze(2).to_broadcast([128, 8, 16, 16]), op=ALU.mult),
                             R=[oh, sif], W=[oh])
                        B.op("dve", lambda e, eo=eo: e.tensor_reduce(eo[:], oh[:], axis=AX.X, op=ALU.add), R=[oh], W=[eo])
                    B.op("dve", lambda e: e.scalar_tensor_tensor(e1[:].rearrange("p h k -> p (h k)"), e1[:].rearrange("p h k -> p (h k)"), 128.0,
                                                                 e2[:].rearrange("p h k -> p (h k)"), op0=ALU.mult, op1=ALU.add), R=[e1, e2], W=[e1])
                    B.op("dve", lambda e: e.tensor_scalar(e1[:], e1[:], float(l * nexp), eoff[:, 0:1], op0=ALU.add, op1=ALU.add), R=[e1, eoff], W=[e1])
                    B.op("dve", lambda e: e.tensor_copy(eid[:], e1[:].rearrange("p h k -> p (h k)")), R=[e1], W=[eid])
                    B.op("dve", lambda e: e.tensor_tensor(ex[:], sc[:], sc[:, :, 0:1].to_broadcast([128, 8, 16]), op=ALU.subtract), R=[sc], W=[ex])
                    B.op("act", lambda e: e.activation(ex[:], ex[:], AF.Exp), R=[ex], W=[ex])
                    B.op("dve", lambda e: e.tensor_reduce(zz[:], ex[:], axis=AX.X, op=ALU.add), R=[ex], W=[zz])
                    B.op("dve", lambda e: e.reciprocal(zz[:], zz[:]), R=[zz], W=[zz])
                    B.op("dve", lambda e: e.tensor_tensor(gg[:].rearrange("p (h k) -> p h k", h=8), ex[:], zz[:].unsqueeze(2).to_broadcast([128, 8, 16]), op=ALU.mult),
                         R=[ex, zz], W=[gg])
                    junk = S2[:, 0:8, :].rearrange("p a k -> p (a k)")
                    pend = None
                    for hk in range(128):
                        ug = Ug[ng % NUG]
                        ng += 1
                        h_ = hs[hk % 4]; w_ = ws[hk % 4]
                        B.dma(ug[:], puv, R=[eid], W=[ug], Q="pool", ind=bass.IndirectOffsetOnAxis(ap=eid[:, hk:hk + 1], axis=0), bc=EMAX)
                        B.op("dve", lambda e, ug=ug, h_=h_: e.scalar_tensor_tensor(junk, ug[:, 0:D], 1.0, x_[:], op0=ALU.mult, op1=ALU.mult,
                                                                                 accum_out=h_[:, 0:1]), R=[ug, x_], W=[S2, h_])
                        B.op("act", lambda e, h_=h_, w_=w_: e.activation(w_[:], h_[:], AF.Gelu), R=[h_], W=[w_])
                        dg = Dg[hk % 3]

                        def tail(hk=hk, dg=dg, vb=ug, w_=w_):
                            B.op("dve", lambda e: e.tensor_scalar(dg[:], ident[:], w_[:, 0:1], gg[:, hk:hk + 1], op0=ALU.mult, op1=ALU.mult),
                                 R=[ident, w_, gg], W=[dg])
                            for hf in range(2):
                                B.op("pe", lambda e, hf=hf: e.matmul(pq[hf][:], lhsT=dg[:], rhs=vb[:, D + hf * 512:D + (hf + 1) * 512],
                                                                     start=(hk == 0), stop=(hk == 127)), R=[dg, vb], W=[pq[hf]])
                        if pend is not None:
                            pend()
                        pend = tail
                    pend()
                    pend = None
                    r_ = rr[t % 2]
                    for hf in range(2):
                        B.op("dve", lambda e, r_=r_, hf=hf: e.scalar_tensor_tensor(r_[:, hf * 512:(hf + 1) * 512], x_[:, hf * 512:(hf + 1) * 512], ALPHA, pq[hf][:],
                                                                                 op0=ALU.mult, op1=ALU.add), R=[x_, pq[hf]], W=[r_])
                    layernorm(ph, r_, g2, b2, r_, D, lns)
                    if last:
                        B.dma(y_out[t * 128:(t + 1) * 128, :], r_[:], R=[r_], W=[B.dr("y", t)])
                    else:
                        B.dma(x2_d[t * 128:(t + 1) * 128, :], r_[:], R=[r_], W=[B.dr("x2", t)])
                        to_fm(ph, r_, psT, x2cc, j)
                if not last:
                    B.dma(xT_d[:, :, cs].rearrange("k p t -> p k t"), x2cc[:], R=[x2cc], W=[B.dr("xT", c)])
            B.barrier()

    phase_x0()
    for l in range(depth):
        if stop_after == "x0":
            break
        phase_inproj(l)
        if stop_after == "inproj":
            break
        if "attn" not in SKIP:
            phase_attn(l)
        if stop_after == "attn":
            break
        phase_s5(l)
        if stop_after == "s5":
            break
        phase_merge(l, x_in if l == 0 else x2_d, None if l == 0 else "x2")
        if stop_after == "merge":
            break
        phase_peer(l, l == depth - 1)
    B.barrier()
    B.es.close()
    return nc


def host_consts(Lc, nseg):
    seglen = Lc // nseg
    t = np.arange(Lc)
    seg = (t // seglen).astype(np.float64)
    tl = (t % seglen)
    a = (tl >> 6).astype(np.float64)
    b = (tl & 63).astype(np.float64)
    qaug = np.zeros((NH, 6, Lc), np.float32)
    kaug = np.zeros((NH, 2, 6, Lc), np.float32)
    negb = np.zeros((128, NH, 128), np.float32)
    i = np.arange(128)
    for h in range(NH):
        s = SLOPES[h]
        qaug[h, 0] = -s * 64 * a
        qaug[h, 1] = -s * b
        qaug[h, 2] = -s * SEG_OFF * seg
        qaug[h, 3:6] = 1.0
        kaug[h, 0, 0:3] = 1.0
        kaug[h, 0, 3] = s * 64 * a
        kaug[h, 0, 4] = s * b
        kaug[h, 0, 5] = s * SEG_OFF * seg
        kaug[h, 1] = -kaug[h, 0]
        negb[:, h, :] = -s * np.abs(i[:, None] - i[None, :])
    bmask = np.zeros((128, 2), np.float32)
    p = np.arange(128)
    bmask[:, 0] = ((p // 16) % 2 == 0)
    bmask[:, 1] = ((p // 16) % 2 == 1)
    return {"qaug": qaug.astype(ml_dtypes.bfloat16), "kaug": kaug.astype(ml_dtypes.bfloat16), "negb": negb, "bmask": bmask}


def host_weights(inp):
    f = lambda k: np.ascontiguousarray(np.asarray(inp[k], dtype=np.float32))
    out = {}
    out["b_gate"] = np.ascontiguousarray(f("b_gate").reshape(DEPTH, 3, 8, 128).transpose(0, 3, 1, 2))
    out["s5_d"] = np.ascontiguousarray(f("s5_d").reshape(DEPTH, 4, 128).transpose(0, 2, 1))
    out["s5_glu_b"] = np.ascontiguousarray(f("s5_glu_b").reshape(DEPTH, 4, 128).transpose(0, 2, 1))
    for k in ("w_in", "att_norm_g", "sgu_ln_g", "sgu_ln_b", "sgu_w_s", "s5_glu_w", "w_branch",
              "w_o", "ln1_g", "ln1_b", "peer_w_q", "ln2_g", "ln2_b"):
        out[k] = f(k)
    out["peer_uv"] = np.concatenate([f("peer_u"), f("peer_v")], axis=-1)
    out["lam4"] = np.ascontiguousarray(np.stack([f("lambda_q1"), f("lambda_k1"), f("lambda_q2"), f("lambda_k2")], axis=1))
    out["sgu_b_s"] = f("sgu_b_s").reshape(DEPTH, 512)
    out["peer_keys"] = f("peer_keys").reshape(DEPTH, 16, 128, 128)
    are, aim, ls = f("s5_a_re"), f("s5_a_im"), f("s5_log_step")
    lsb = np.broadcast_to(ls[..., None], are.shape)
    def st_layout(a):
        a = a.reshape(DEPTH, 2, 16, 2, 64)
        return a.transpose(0, 3, 4, 1, 2).reshape(DEPTH, 128, 32)
    out["s5p"] = np.ascontiguousarray(np.stack([st_layout(are), st_layout(aim), st_layout(lsb)], axis=2))
    def ch_layout(a):
        a = a.reshape(DEPTH, 2, 4, 8, 64, 16)
        return a.transpose(0, 3, 5, 1, 2, 4).reshape(DEPTH, 128, 512)
    bc16 = lambda a: np.broadcast_to(a[..., None], a.shape + (16,))
    out["s5q"] = np.ascontiguousarray(np.stack([ch_layout(bc16(are)), ch_layout(bc16(aim)), ch_layout(bc16(lsb))], axis=2))
    out["s5b"] = np.ascontiguousarray(np.stack([ch_layout(f("s5_b_re")), ch_layout(f("s5_b_im"))], axis=2))
    cc = np.zeros((DEPTH, 2, 64, 2, 2, 16, 2, 16), np.float32)
    for ri, key in enumerate(("s5_c_re", "s5_c_im")):
        c = f(key).reshape(DEPTH, 2, 16, 2, 16, 64)
        for gl in range(2):
            cc[:, gl, :, ri, :, :, gl, :] = c[:, :, :, gl, :, :].transpose(0, 4, 1, 2, 3)
    out["s5c"] = np.ascontiguousarray(cc.reshape(DEPTH, 128, 2, 32, 32))
    return out


_CACHE = {}


def kernel(**inputs):
    xp = np.asarray(inputs["x_prompt"], dtype=np.float32)
    xs = np.asarray(inputs["x_sample"], dtype=np.float32)
    Lc = xp.shape[1]
    wts = host_weights(inputs)
    cp = host_consts(Lc, 1)
    cs = host_consts(Lc, 2)
    one = np.ones((128, 1), np.float32)
    zero = np.zeros((128, 1), np.float32)
    streams = [xp[0], xp[1], xs[0:2].reshape(Lc, D), xs[2:4].reshape(Lc, D)]
    in_maps = []
    for c in range(8):
        si = c if c < 4 else 2 + (c % 2)
        m = dict(wts)
        m["x"] = np.ascontiguousarray(streams[si])
        m.update(cp if si < 2 else cs)
        m["gate"] = one if si < 2 else zero
        m["eoff"] = zero if c < 4 else np.full((128, 1), 1.0e6, np.float32)
        in_maps.append(m)
    if "nc" not in _CACHE:
        _CACHE["nc"] = build(Lc)
    res = run_bass_kernel_spmd(_CACHE["nc"], in_maps, core_ids=list(range(8)))
    r = res.results
    y_prompt = np.stack([r[0]["y"], r[1]["y"]], axis=0).astype(np.float32)
    y_sample = np.concatenate([r[2]["y"].reshape(2, Lc // 2, D), r[3]["y"].reshape(2, Lc // 2, D)], axis=0).astype(np.float32)
    return (y_prompt, y_sample)
```
